# Optimizing a Trainium2 kernel written in Bass

```python
import jax, jax.numpy as jnp
from jax import lax
import numpy as np

D_MODEL = 1024
BATCH = 8
SEQ = 4096
DEPTH = 1

CHUNK = 64
HEAD_DIM = 64
MIX_WIDTH = D_MODEL
A_HEADS = (MIX_WIDTH // 2) // HEAD_DIM
A_KV_HEADS = 2
A_GROUP = A_HEADS // A_KV_HEADS
A_WINDOW = 128
A_BAND_CHUNKS = -(-A_WINDOW // CHUNK) + 1
B_HEADS = (MIX_WIDTH - A_HEADS * HEAD_DIM) // HEAD_DIM
B_LEFT_CHUNKS = 8
B_BAND_CHUNKS = B_LEFT_CHUNKS + 1
B_MAX_REL = 128
D_FF = 4 * D_MODEL
EPS = 1e-6
NEG_INF = -1e30

QA_W = A_HEADS * HEAD_DIM
KA_W = A_KV_HEADS * HEAD_DIM
QB_W = B_HEADS * HEAD_DIM
PROJ_W = QA_W + 2 * KA_W + 3 * QB_W

kernel_name = "hybrid_swa_sink_chunk_relpos_block"


def rms_norm(x, g):
    xf = x.astype(jnp.float32)
    y = xf * lax.rsqrt(jnp.mean(xf * xf, axis=-1, keepdims=True) + EPS)
    return (y * g.astype(jnp.float32)).astype(x.dtype)


def alibi_slopes(n_heads):
    return jnp.exp2(-8.0 * (jnp.arange(n_heads, dtype=jnp.float32) + 1.0) / n_heads)


def chunk_band(t, n_band):
    b, s, h, d = t.shape
    nc = s // CHUNK
    tc = t.reshape(b, nc, CHUNK, h, d)
    tc = jnp.pad(tc, ((0, 0), (n_band - 1, 0), (0, 0), (0, 0), (0, 0)))
    idx = jnp.arange(nc)[:, None] + jnp.arange(n_band)[None, :]
    band = tc[:, idx]
    return band.reshape(b, nc, n_band * CHUNK, h, d)


def band_valid(nc, n_band):
    kc = jnp.arange(nc)[:, None] - (n_band - 1) + jnp.arange(n_band)[None, :]
    return jnp.repeat(kc >= 0, CHUNK, axis=1)


def band_distance(n_band):
    i = jnp.arange(CHUNK)[:, None]
    j = jnp.arange(n_band * CHUNK)[None, :]
    return (n_band - 1) * CHUNK + i - j


def mixer_a(q, k, v, sinks):
    b, s = q.shape[:2]
    nc = s // CHUNK
    qc = q.reshape(b, nc, CHUNK, A_KV_HEADS, A_GROUP, HEAD_DIM)
    kb = chunk_band(k, A_BAND_CHUNKS)
    vb = chunk_band(v, A_BAND_CHUNKS)
    scores = jnp.einsum('bnqhgd,bnshd->bnhgqs', qc, kb).astype(jnp.float32) * (HEAD_DIM ** -0.5)
    dist = jnp.abs(band_distance(A_BAND_CHUNKS)).astype(jnp.float32)
    slopes = alibi_slopes(A_HEADS).reshape(A_KV_HEADS, A_GROUP)
    scores = scores - slopes[:, :, None, None] * dist
    valid = band_valid(nc, A_BAND_CHUNKS)
    scores = jnp.where(valid[None, :, None, None, None, :], scores, NEG_INF)
    sink = sinks.astype(jnp.float32).reshape(A_KV_HEADS, A_GROUP)[None, None, :, :, None, None]
    m = jnp.maximum(jnp.max(scores, axis=-1, keepdims=True), sink)
    p = jnp.exp(scores - m)
    probs = (p / (jnp.sum(p, axis=-1, keepdims=True) + jnp.exp(sink - m))).astype(v.dtype)
    out = jnp.einsum('bnhgqs,bnshd->bnqhgd', probs, vb)
    return out.reshape(b, s, QA_W)


def mixer_b(q, k, v, rel_bias):
    b, s = q.shape[:2]
    nc = s // CHUNK
    qc = q.reshape(b, nc, CHUNK, B_HEADS, HEAD_DIM)
    kb = chunk_band(k, B_BAND_CHUNKS)
    vb = chunk_band(v, B_BAND_CHUNKS)
    scores = jnp.einsum('bnqhd,bnshd->bnhqs', qc, kb).astype(jnp.float32) * (HEAD_DIM ** -0.5)
    rel = jnp.clip(band_distance(B_BAND_CHUNKS), -B_MAX_REL, B_MAX_REL) + B_MAX_REL
    bias = rel_bias.astype(jnp.float32)[:, rel]
    valid = band_valid(nc, B_BAND_CHUNKS)
    scores = jnp.where(valid[None, :, None, None, :], scores + bias, NEG_INF)
    probs = jax.nn.softmax(scores, axis=-1).astype(v.dtype)
    out = jnp.einsum('bnhqs,bnshd->bnqhd', probs, vb)
    return out.reshape(b, s, QB_W)


def setup_inputs(seed: int = 0) -> dict:
    key = jax.random.key(seed)
    ks = jax.random.split(key, 13)
    f32 = jnp.float32

    def nrm(k, shape, scale):
        return jax.random.normal(k, shape, f32) * scale

    return {
        "x": nrm(ks[0], (BATCH, SEQ, D_MODEL), 1.0),
        "norm1_g": 1.0 + nrm(ks[1], (DEPTH, D_MODEL), 0.02),
        "w_in": nrm(ks[2], (DEPTH, D_MODEL, PROJ_W), D_MODEL ** -0.5),
        "sinks_a": nrm(ks[3], (DEPTH, A_HEADS), 0.5),
        "rel_bias_b": nrm(ks[4], (DEPTH, B_HEADS, 2 * B_MAX_REL + 1), 0.1),
        "out_norm_a_g": 1.0 + nrm(ks[5], (DEPTH, QA_W), 0.02),
        "out_norm_b_g": 1.0 + nrm(ks[6], (DEPTH, QB_W), 0.02),
        "w_out": nrm(ks[7], (DEPTH, MIX_WIDTH, D_MODEL), MIX_WIDTH ** -0.5),
        "norm2_g": 1.0 + nrm(ks[8], (DEPTH, D_MODEL), 0.02),
        "w_ff1": nrm(ks[9], (DEPTH, D_MODEL, D_FF), D_MODEL ** -0.5),
        "w_ff2": nrm(ks[10], (DEPTH, D_FF, D_MODEL), D_FF ** -0.5),
        "final_norm_g": 1.0 + nrm(ks[11], (D_MODEL,), 0.02),
    }


def reference(x, norm1_g, w_in, sinks_a, rel_bias_b, out_norm_a_g, out_norm_b_g,
              w_out, norm2_g, w_ff1, w_ff2, final_norm_g):
    b, s, _ = x.shape
    split_at = np.cumsum([QA_W, KA_W, KA_W, QB_W, QB_W])
    h = x
    for layer in range(DEPTH):
        n = rms_norm(h, norm1_g[layer])
        proj = jnp.einsum('bsd,dp->bsp', n, w_in[layer])
        qa, ka, va, qb, kb, vb = jnp.split(proj, split_at, axis=-1)
        ya = mixer_a(qa.reshape(b, s, A_HEADS, HEAD_DIM),
                     ka.reshape(b, s, A_KV_HEADS, HEAD_DIM),
                     va.reshape(b, s, A_KV_HEADS, HEAD_DIM),
                     sinks_a[layer])
        yb = mixer_b(qb.reshape(b, s, B_HEADS, HEAD_DIM),
                     kb.reshape(b, s, B_HEADS, HEAD_DIM),
                     vb.reshape(b, s, B_HEADS, HEAD_DIM),
                     rel_bias_b[layer])
        y = jnp.concatenate([rms_norm(ya, out_norm_a_g[layer]),
                             rms_norm(yb, out_norm_b_g[layer])], axis=-1)
        h = h + jnp.einsum('bsm,md->bsd', y, w_out[layer])
        n2 = rms_norm(h, norm2_g[layer])
        u = jnp.square(jax.nn.relu(jnp.einsum('bsd,df->bsf', n2, w_ff1[layer])))
        h = h + jnp.einsum('bsf,fd->bsd', u, w_ff2[layer])
    return rms_norm(h, final_norm_g)
```

```python
from contextlib import ExitStack

import numpy as np
import concourse.bass as bass
import concourse.mybir as mybir
from concourse.bass_utils import run_bass_kernel_spmd

F32 = mybir.dt.float32
BF16 = mybir.dt.bfloat16
AF = mybir.ActivationFunctionType
ALU = mybir.AluOpType


def MM(out, lhsT, rhs, start, stop):
    return lambda e: e.matmul(out=out, lhsT=lhsT, rhs=rhs, start=start, stop=stop)


def TR(out, in_, identity):
    return lambda e: e.transpose(out=out, in_=in_, identity=identity)


def ACTF(out, in_, func, **kw):
    return lambda e: e.activation(out=out, in_=in_, func=func, **kw)


def TS(out, in0, scalar1, scalar2, op0, op1=None):
    if op1 is None:
        return lambda e: e.tensor_scalar(out=out, in0=in0, scalar1=scalar1, scalar2=scalar2, op0=op0)
    return lambda e: e.tensor_scalar(out=out, in0=in0, scalar1=scalar1, scalar2=scalar2, op0=op0, op1=op1)


def TT(out, in0, in1, op):
    return lambda e: e.tensor_tensor(out=out, in0=in0, in1=in1, op=op)


def STT(out, in0, scalar, in1, op0, op1):
    return lambda e: e.scalar_tensor_tensor(out=out, in0=in0, scalar=scalar, in1=in1, op0=op0, op1=op1)


def CP(out, in_):
    return lambda e: e.tensor_copy(out=out, in_=in_)


def RCP(out, in_):
    return lambda e: e.reciprocal(out=out, in_=in_)


def DMA(out, in_):
    return lambda e: e.dma_start(out=out, in_=in_)


def MS(ap, val):
    return lambda e: e.memset(ap, val)


S = 4096
D = 1024
NG = 8
DFF = 4096
PROJ = 2304
EPS = 1e-6
NEG = -30000.0
C_QA, C_KA, C_VA, C_QB, C_KB, C_VB = 0, 512, 640, 768, 1280, 1792

ENGS = ("pe", "act", "dve", "pool", "sp")
EPOCH = 16000


class Res:
    __slots__ = ("name", "last_w", "readers")

    def __init__(self, name):
        self.name = name
        self.last_w = None
        self.readers = []


class Op:
    __slots__ = ("eng", "fn", "deps", "sig", "dma_key", "dma_val", "has_dep", "tag")

    def __init__(self, eng, fn):
        self.eng = eng
        self.fn = fn
        self.deps = []
        self.sig = None
        self.dma_key = None
        self.dma_val = None
        self.has_dep = False


class Prog:
    def __init__(self, nc):
        self.nc = nc
        self.ops = {e: [] for e in ENGS}
        self.dma_cnt = {}
        self.res = {}
        self.stage = ""

    def R(self, name):
        r = self.res.get(name)
        if r is None:
            r = Res(name)
            self.res[name] = r
        return r

    def _rs(self, lst):
        out = []
        for x in lst:
            if x is None:
                continue
            if isinstance(x, str):
                out.append(self.R(x))
            else:
                out.extend(self._rs(x))
        return out

    def op(self, eng, fn, reads=(), writes=(), dma=None):
        o = Op(eng, fn)
        o.tag = self.stage
        reads = self._rs(reads)
        writes = self._rs(writes)
        deps = {}

        def add(p, kind):
            if p is None or p is o:
                return
            if p.dma_key is None and p.eng == eng and eng == "pe":
                return
            deps[id(p)] = p

        for r in reads:
            add(r.last_w, "raw")
        for w in writes:
            add(w.last_w, "waw")
            for rd in w.readers:
                add(rd, "war")
        for r in reads:
            if dma is None:
                r.readers = [q for q in r.readers if q.dma_key is not None or q.eng != eng]
            r.readers.append(o)
        for w in writes:
            w.last_w = o
            w.readers = []
        o.deps = list(deps.values())
        for p in o.deps:
            p.has_dep = True
        if dma is not None:
            c = self.dma_cnt.get(dma, 0) + 16
            self.dma_cnt[dma] = c
            o.dma_key = dma
            o.dma_val = c
        self.ops[eng].append(o)
        return o

    def emit(self, stack):
        nc = self.nc
        nsig = {}
        for e in ENGS:
            k = 0
            for o in self.ops[e]:
                if o.dma_key is None and o.has_dep:
                    k += 1
                    o.sig = k
            nsig[e] = k
        esem = {}
        for e in ENGS:
            ne = (nsig[e] + EPOCH - 1) // EPOCH
            esem[e] = [stack.enter_context(nc.semaphore(f"s_{e}{i}")) for i in range(ne)]
        dsem = {k: stack.enter_context(nc.semaphore(f"d_{k}")) for k in self.dma_cnt}
        self.stats = {e: [len(self.ops[e]), nsig[e], 0] for e in ENGS}
        block = stack.enter_context(nc.Block())

        def run(e, engine):
            waited = {}
            for o in self.ops[e]:
                need = {}
                for p in o.deps:
                    if p.dma_key is not None:
                        key = ("d", p.dma_key)
                        val = p.dma_val
                    else:
                        key = ("e", p.eng)
                        val = p.sig
                    if need.get(key, 0) < val:
                        need[key] = val
                for key, val in need.items():
                    if waited.get(key, 0) >= val:
                        continue
                    waited[key] = val
                    if key[0] == "d":
                        sem = dsem[key[1]]
                        wv = val
                    else:
                        ep = (val - 1) // EPOCH
                        sem = esem[key[1]][ep]
                        wv = val - ep * EPOCH
                    engine.wait_ge(sem, wv)
                    self.stats[e][2] += 1
                if o.fn is None:
                    continue
                ins = o.fn(engine)
                if o.dma_key is not None:
                    ins.then_inc(dsem[o.dma_key], 16)
                elif o.sig is not None:
                    ep = (o.sig - 1) // EPOCH
                    ins.then_inc(esem[e][ep], 1)

        @block.tensor
        def _(eng):
            run("pe", eng)

        @block.scalar
        def _(eng):
            run("act", eng)

        @block.vector
        def _(eng):
            run("dve", eng)

        @block.gpsimd
        def _(eng):
            run("pool", eng)

        @block.sync
        def _(eng):
            run("sp", eng)


def build_nc(n_groups=NG, debug=False, totals=None):
    nc = bass.Bass("TRN2", target_bir_lowering=False)
    dram = lambda n, s, d=F32, kind="ExternalInput": nc.dram_tensor(n, s, d, kind=kind).ap()
    x_d = dram("x", [S, D])
    win_d = dram("w_in", [D, PROJ])
    wout_d = dram("w_out", [D, D])
    w1_d = dram("w_ff1", [D, DFF])
    w2_d = dram("w_ff2", [DFF, D])
    gT_d = dram("gT", [128, 24])
    gf_d = dram("gf", [1, D])
    sinks_d = dram("sinks", [1, 8])
    tbraw_d = dram("tbraw", [8, 128, 256])
    ta_d = dram("ta", [8, 128, 256])
    cst_d = dram("cst", [128, 128 + 256 + 256 + 64])
    out_d = dram("out", [S, D], F32, kind="ExternalOutput")
    w1bf_d = dram("w1bf", [16, 128, 8, 256], BF16, kind="Internal")
    w2bf_d = dram("w2bf", [16, 128, 4, 512], BF16, kind="Internal")
    dbg = {}
    if debug:
        dbg["qp"] = dram("dbg_qp", [8, 128, 512], F32, kind="ExternalOutput")
        dbg["kt"] = dram("dbg_kt", [128, 5, 1024], F32, kind="ExternalOutput")
        dbg["v"] = dram("dbg_v", [128, 8 * 10 * 65], F32, kind="ExternalOutput")
        dbg["y"] = dram("dbg_y", [128, 4 * 1024], F32, kind="ExternalOutput")
        dbg["h"] = dram("dbg_h", [128, 4 * 1024], F32, kind="ExternalOutput")

    with ExitStack() as st:
        sb = lambda n, s, d: st.enter_context(nc.sbuf_tensor(n, s, d))
        win_sb = sb("win_sb", [128, 8, PROJ], BF16)
        wout_sb = sb("wout_sb", [128, 8, D], BF16)
        wst = [sb(f"wst{i}", [128, 2048], BF16) for i in range(4)]
        xh = [sb(f"xh{i}", [128, 4, D], F32) for i in range(2)]
        xn = [sb(f"xn{i}", [128, D], BF16) for i in range(2)]
        xtmp = [sb(f"xtmp{i}", [128, D], F32) for i in range(2)]
        nT = sb("nT", [128, 8, 512], BF16)
        n2T = sb("n2T", [128, 8, 512], BF16)
        QP = [sb(f"qp{i}", [128, 512], BF16) for i in range(8)]
        KT = sb("KT", [128, 5, 1024], BF16)
        V = sb("V", [128, 8, 10, 65], BF16)
        PT = [sb(f"pt{i}", [128, 2560], BF16) for i in range(2)]
        ybuf = sb("ybuf", [128, 4, D], BF16)
        uT = sb("uT", [128, 16, 512], BF16)
        rtmp = [sb(f"rtmp{i}", [128, 512], BF16) for i in range(2)]
        TA = sb("TA", [128, 8, 256], BF16)
        TB = sb("TB", [128, 8, 256], BF16)
        ident = sb("ident", [128, 128], BF16)
        mask9 = sb("mask9", [128, 64], BF16)
        gf = sb("gf_sb", [128, D], F32)
        gT = sb("gT_sb", [128, 24], F32)
        esink = sb("esink", [128, 8], F32)
        mhalf = sb("mhalf", [128, 8], F32)
        stat = [sb(f"stat{i}", [128, 40], F32) for i in range(2)]
        rl = [sb(f"rl{i}", [128, 4], F32) for i in range(2)]
        bank = [st.enter_context(nc.psum_tensor(f"bank{i}", [128, 512], F32)) for i in range(8)]
        PA = [0, 1, 2, 3]
        PB = [4, 5, 6, 7]

        P = Prog(nc)
        op = P.op

        WBLK = {"KAV": (512, 768), "KB": (1280, 1792), "VB": (1792, 2304), "QA": (0, 512), "QB": (768, 1280)}

        def win_load(name):
            c0, c1 = WBLK[name]
            op("pool", DMA(win_sb[:, :, c0:c1], win_d[:, c0:c1].rearrange("(k p) c -> p k c", p=128)),
               writes=[f"win_{name}"], dma=f"c_win_{name}")

        def win_res(col0):
            for name, (c0, c1) in WBLK.items():
                if c0 <= col0 < c1:
                    return f"win_{name}"
            raise KeyError(col0)

        tbtmp = uT[:].rearrange("p a b -> p (a b)").bitcast(F32)
        UT_ALL = [f"uT{c}" for c in range(16)]
        tbraw = tbtmp[:, 0:2048].rearrange("p (h u) -> p h u", u=256)
        vneg = tbtmp[:, 2048:2560]

        def setup_early():
            op("sp", DMA(gT[:], gT_d), writes=["gT"], dma="c_gT")
            op("pool", MS(mhalf[:], -0.5), writes=["mhalf"])
            op("pool", DMA(ident[:], cst_d[:, 0:128]), writes=["ident"], dma="c_id")
            win_load("KAV")
            win_load("KB")

        def setup_mid():
            x_loads(0)
            win_load("VB")
            win_load("QA")
            op("pool", MS(V[:].rearrange("p a b c -> p (a b c)"), 1.0),
               writes=[f"V{i}{sfx}" for i in range(8) for sfx in ("_a", "_b")])
            for i in range(8):
                op("pool", MS(QP[i][:], 0.0), writes=[f"qp{i}"])
            op("pool", DMA(TA[:], ta_d.rearrange("h p u -> p h u")), writes=["TA"], dma="c_ta")
            op("pool", DMA(mask9[:], cst_d[:, 640:704]), writes=["mask9"], dma="c_m9")
            win_load("QB")
            op("sp", DMA(esink[:], sinks_d.partition_broadcast(128)), writes=["esink"], dma="c_sink")
            op("sp", DMA(gf[:], gf_d.partition_broadcast(128)), writes=["gf"], dma="c_gf")
            op("sp", DMA(tbraw, tbraw_d.rearrange("h p u -> p h u")), writes=UT_ALL, dma="c_tb")
            op("sp", DMA(vneg, cst_d[:, 128:640]), writes=["vneg"], dma="c_vn")

        def setup_tb():
            for h in range(8):
                op("dve", TS(tbraw[:, h, :], tbraw[:, h, :], tbraw[:, h, 255:256], None, ALU.subtract),
                   reads=UT_ALL, writes=UT_ALL)
                op("dve", TT(tbraw[:, h, :], tbraw[:, h, :], vneg[:, 0:256], ALU.mult),
                   reads=UT_ALL + ["vneg"], writes=UT_ALL)
                op("dve", TT(TB[:, h, :], tbraw[:, h, :], vneg[:, 256:512], ALU.add),
                   reads=UT_ALL + ["vneg"], writes=["TB"])
            op("act", ACTF(esink[:], esink[:], AF.Exp), reads=["esink"], writes=["esink"])

        def setup_late():
            for k in range(8):
                op("pool", DMA(wout_sb[:, k, :], wout_d[k * 128:(k + 1) * 128, :]), writes=[f"wout{k}"], dma=f"c_wout{k}")

        stream = []
        for ffh in range(2):
            for j in range(8):
                stream.append(("w1", ffh * 8 + j))
            for dmh in range(2):
                for q in range(4):
                    stream.append(("w2", (ffh * 2 + dmh) * 4 + q))
        NST = len(stream)
        st_state = {"next": 0}

        def stream_ensure(upto):
            upto = min(upto, NST * n_groups)
            while st_state["next"] < upto:
                s_ = st_state["next"]
                kind, idx = stream[s_ % NST]
                slot = s_ % 4
                if kind == "w1":
                    scr = w1bf_d[idx].rearrange("p k c -> p (k c)")
                    rd = f"w1bf{idx}"
                    src32 = w1_d[:, idx * 256:(idx + 1) * 256].rearrange("(k p) c -> p k c", p=128)
                    dst3 = wst[slot][:].rearrange("p (k c) -> p k c", c=256)
                else:
                    scr = w2bf_d[idx].rearrange("p r c -> p (r c)")
                    rd = f"w2bf{idx}"
                    ffh, dmh, q = idx // 8, (idx // 4) % 2, idx % 4
                    r0 = (ffh * 16 + q * 4) * 128
                    src32 = w2_d[r0:r0 + 512, dmh * 512:(dmh + 1) * 512].rearrange("(r p) c -> p r c", p=128)
                    dst3 = wst[slot][:].rearrange("p (r c) -> p r c", c=512)
                if s_ < NST:
                    op("pool", DMA(dst3, src32), writes=[f"wst{slot}"], dma=f"wstc{slot}")
                    if n_groups > 1:
                        op("sp", DMA(scr, wst[slot][:]), reads=[f"wst{slot}"], writes=[rd], dma=f"wb{slot}")
                else:
                    op("sp", DMA(wst[slot][:], scr), reads=[rd], writes=[f"wst{slot}"], dma=f"wst{slot}")
                st_state["next"] += 1

        rot = {"A": 0, "xn": 0}
        NXN = len(xn)

        def next_bankA():
            b = PA[rot["A"] % 4]
            rot["A"] += 1
            return b

        xn_busy = set()

        def xn_slot():
            for _ in range(NXN):
                s_ = rot["xn"] % NXN
                rot["xn"] += 1
                if s_ not in xn_busy:
                    xn_busy.add(s_)
                    return s_
            raise AssertionError("no free xn slot")

        def XN(s_):
            return [f"xn{s_}_0", f"xn{s_}_1"]

        def XH(par, tb):
            return [f"xh{par}_{tb}_0", f"xh{par}_{tb}_1"]

        def NTR(name, ks, tbs):
            return [f"{name}{k}_{t}" for k in ks for t in tbs]

        def rstd_from_ssq(sq_ap, r_ap, n, res_sq, res_r):
            k = sq_ap.shape[1]
            op("pool", TS(sq_ap, sq_ap, 1.0 / n, EPS, ALU.mult, ALU.add), reads=[res_sq], writes=[res_sq])
            op("pool", TT(r_ap, sq_ap, mhalf[:, 0:k], ALU.pow), reads=[res_sq, "mhalf"], writes=[res_r])

        def transpose_block(src_bf, src_res, dst, dname, tb, gcol):
            b = PA[rot["A"] % 2]
            rot["A"] += 1
            pb = bank[b][:].bitcast(BF16)
            for k in range(8):
                op("pe", TR(pb[:, k * 128:(k + 1) * 128], src_bf[:, k * 128:(k + 1) * 128], ident[:]),
                   reads=[src_res, "ident"], writes=[f"bank{b}"])
            for k in range(8):
                op("dve", TS(dst[:, k, tb * 128:(tb + 1) * 128], pb[:, k * 128:(k + 1) * 128],
                             gT[:, gcol + k:gcol + k + 1], None, ALU.mult),
                   reads=[f"bank{b}", "gT"], writes=[f"{dname}{k}_{tb}"])

        out_dmas = []
        ALLTB = range(4)

        xloaded = set()
        n2T_gen = {"g": -1}

        def x_loads(g):
            par = g % 2
            for tb in range(4):
                xloaded.add((g, tb))
                op("sp", DMA(xh[par][:, tb, :], x_d[g * 512 + tb * 128:g * 512 + (tb + 1) * 128, :]),
                   writes=XH(par, tb), dma=f"x{par}{tb}")

        def FN_a(g):
            par = g % 2
            sg = stat[par]
            junk = uT[:, 0:2, :].rearrange("p a b -> p (a b)")
            for tb in range(4):
                op("act", ACTF(junk, xh[par][:, tb, :], AF.Square, accum_out=sg[:, 32 + tb:33 + tb]),
                   reads=XH(par, tb), writes=["uT0", "uT1", f"st{par}_g{tb}"])

        def FN_b(g):
            par = g % 2
            sg = stat[par]
            for tb in range(4):
                rstd_from_ssq(sg[:, 32 + tb:33 + tb], sg[:, 36 + tb:37 + tb], D, f"st{par}_g{tb}", f"st{par}_h{tb}")

        def FN_c(g, tbs=None):
            par = g % 2
            sg = stat[par]
            xg = xh[par]
            t0 = g * 512
            for tb in (range(4) if tbs is None else tbs):
                op("dve", STT(xg[:, tb, :], xg[:, tb, :], sg[:, 36 + tb:37 + tb], gf[:], ALU.mult, ALU.mult),
                   reads=XH(par, tb) + [f"st{par}_h{tb}", "gf"], writes=XH(par, tb))

        def FN_d(g):
            par = g % 2
            xg = xh[par]
            t0 = g * 512
            for tb in range(4):
                o = op("sp", DMA(out_d[t0 + tb * 128:t0 + (tb + 1) * 128, :], xg[:, tb, :]),
                       reads=XH(par, tb), dma=f"o{par}{tb}")
                out_dmas.append(o)

        def P1(g):
            par = g % 2
            xg = xh[par]
            sg = stat[par]
            t0 = g * 512
            P.stage = f"A{g}"
            slots = {}

            def xt_load(tb):
                xt = tb % 2
                q_ = "sp" if (tb < 2 or g == 0) else "pool"
                op(q_, DMA(xtmp[xt][:], x_d[t0 + tb * 128:t0 + (tb + 1) * 128, :]),
                   writes=[f"xtmp{xt}"], dma=f"xt{xt}_{q_}")

            def preA(tb):
                xs = xn_slot()
                slots[tb] = xs
                xt = tb % 2
                if tb < 2:
                    xt_load(tb)
                op("act", ACTF(xn[xs][:], xtmp[xt][:], AF.Square, accum_out=sg[:, tb:tb + 1]),
                   reads=[f"xtmp{xt}"], writes=XN(xs) + [f"st{par}_a{tb}"])
                rstd_from_ssq(sg[:, tb:tb + 1], sg[:, 4 + tb:5 + tb], D, f"st{par}_a{tb}", f"st{par}_b{tb}")
                if tb % 2 == 0:
                    op("act", ACTF(xn[xs][:], xtmp[xt][:], AF.Copy, scale=sg[:, 4 + tb:5 + tb]),
                       reads=[f"xtmp{xt}", f"st{par}_b{tb}"], writes=XN(xs))
                else:
                    op("pool", TS(xn[xs][:], xtmp[xt][:], sg[:, 4 + tb:5 + tb], 1.0, ALU.mult, ALU.mult),
                       reads=[f"xtmp{xt}", f"st{par}_b{tb}"], writes=XN(xs))
                if tb + 2 < 4:
                    xt_load(tb + 2)

            def TA_(tb):
                xs = slots[tb]
                transpose_block(xn[xs], XN(xs), nT, "nT", tb, 0)
                xn_busy.discard(xs)

            preA(0)
            preA(1)
            for _ in range(3):
                yield 6000
            TA_(0)
            preA(2)
            yield 5000
            TA_(1)
            preA(3)
            yield 5000
            TA_(2)
            yield 4000
            TA_(3)
            yield 2048

            P.stage = f"B{g}"

            def proj_fm(col0, evac):
                b = next_bankA()
                for k in range(8):
                    op("pe", MM(bank[b][:], win_sb[:, k, col0:col0 + 128], nT[:, k, :], k == 0, k == 7),
                       reads=NTR("nT", [k], ALLTB) + [win_res(col0)], writes=[f"bank{b}"])
                evac(b)

            def evac_k(chunk):
                def f(b):
                    op("act", ACTF(KT[:, chunk, par * 512:(par + 1) * 512], bank[b][:], AF.Copy),
                       reads=[f"bank{b}"], writes=[f"KT{par}_{chunk}"])
                return f

            def evac_q(t_lo, t_hi):
                def f(b):
                    op("act", ACTF(QP[t_lo][0:64, :], bank[b][0:64, :], AF.Copy, scale=0.125),
                       reads=[f"bank{b}"], writes=[f"qp{t_lo}"])
                    op("act", ACTF(QP[t_hi][64:128, :], bank[b][64:128, :], AF.Copy, scale=0.125),
                       reads=[f"bank{b}"], writes=[f"qp{t_hi}"])
                return f

            proj_fm(C_KA, evac_k(0))
            yield 4096
            for m in range(4):
                proj_fm(C_KB + m * 128, evac_k(1 + m))
                yield 4096
            for tb in range(4):
                slot = (g * 4 + tb) % 8
                b1 = next_bankA()
                b2 = next_bankA()
                for k in range(8):
                    op("pe", MM(bank[b1][:, 0:128], nT[:, k, tb * 128:(tb + 1) * 128],
                                win_sb[:, k, C_VA:C_VA + 128], k == 0, k == 7),
                       reads=[f"nT{k}_{tb}", "win_KAV"], writes=[f"bank{b1}"])
                for k in range(8):
                    op("pe", MM(bank[b2][:], nT[:, k, tb * 128:(tb + 1) * 128],
                                win_sb[:, k, C_VB:C_VB + 512], k == 0, k == 7),
                       reads=[f"nT{k}_{tb}", "win_VB"], writes=[f"bank{b2}"])
                op("dve", CP(V[:, slot, 0:2, 0:64], bank[b1][:, 0:128].rearrange("p (h d) -> p h d", d=64)),
                   reads=[f"bank{b1}"], writes=[f"V{slot}_a"])
                op("dve", CP(V[:, slot, 2:10, 0:64], bank[b2][:].rearrange("p (h d) -> p h d", d=64)),
                   reads=[f"bank{b2}"], writes=[f"V{slot}_b"])
                yield 5120
            for j in range(4):
                proj_fm(C_QA + j * 128, evac_q(j, 4 + j))
                yield 4096

            def attention(mixer):
                W = 3 if mixer == "A" else 9
                TX = TA if mixer == "A" else TB
                txres = "TA" if mixer == "A" else "TB"
                vsuf = "_a" if mixer == "A" else "_b"
                back = 1 if mixer == "A" else 4
                kbs = [kb for kb in range(4 * g - back, 4 * g + 4) if kb >= 0]
                info = {}
                off = 0
                for ki, kb in enumerate(kbs):
                    cs = max(2 * kb, 8 * g)
                    ce = min(2 * kb + W, 8 * g + 7)
                    u0 = 64 * (cs - 2 * kb)
                    u1 = 64 * (ce + 1 - 2 * kb)
                    info[kb] = (u0, u1, off, 64 * (cs - 8 * g), ki)
                    off += u1 - u0
                sbank = [PA[0], PA[1]]
                obank = [PA[2], PA[3]]
                cnt = {"s": 0}

                def qk(h, fill=()):
                    fill = list(fill)
                    nkb = len(kbs)
                    ptp = h % 2
                    if mixer == "A":
                        qt = h
                        kchunk = 0
                    else:
                        qt = (h // 2) if h % 2 == 0 else 4 + h // 2
                        kchunk = 1 + h // 2
                    for kidx, kb in enumerate(kbs):
                        u0, u1, poff, qc0, ki = info[kb]
                        n = u1 - u0
                        b = sbank[cnt["s"] % 2]
                        cnt["s"] += 1
                        kcol = (kb % 8) * 128
                        kpar = (kb // 4) % 2
                        mms = [(bank[b][:, 0:n], KT[:, kchunk, kcol:kcol + 128], QP[qt][:, qc0:qc0 + n],
                                [f"KT{kpar}_{kchunk}", f"qp{qt}"])]
                        if u0 < 256:
                            ub1 = min(u1, 256)
                            mms.append((bank[b][:, 0:ub1 - u0], ident[:], TX[:, h, u0:ub1], ["ident", txres]))
                        if mixer == "B" and u1 == 640:
                            mms.append((bank[b][:, 576 - u0:640 - u0], ident[:], mask9[:], ["ident", "mask9"]))
                        cost = 0
                        for i, (o_, l_, r_, rd) in enumerate(mms):
                            op("pe", MM(o_, l_, r_, i == 0, i == len(mms) - 1), reads=rd, writes=[f"bank{b}"])
                            cost += o_.shape[1]
                        op("act", ACTF(PT[ptp][:, poff:poff + n], bank[b][:, 0:n], AF.Exp),
                           reads=[f"bank{b}"], writes=[f"pt{ptp}_{ki}"])
                        left = nkb - kidx
                        take = (len(fill) + left - 1) // left
                        for f in fill[:take]:
                            f()
                        fill = fill[take:]
                        yield cost + 100 * take
                    for f in fill:
                        f()

                def pv(h):
                    ptp = h % 2
                    b = obank[h % 2]
                    hv = (h // 4) if mixer == "A" else 2 + h
                    ycol = (0 if mixer == "A" else 512) + h * 64
                    ems = []
                    for qbl in range(4):
                        qb = 4 * g + qbl
                        ks = [kb for kb in range(qb - back, qb + 1) if kb >= 0]
                        for i, kb in enumerate(ks):
                            u0, u1, poff, qc0, ki = info[kb]
                            c0 = poff + 128 * (qb - kb) - u0
                            slot = kb % 8

                            def em(qbl=qbl, c0=c0, slot=slot, ki=ki, first=(i == 0), last=(i == len(ks) - 1)):
                                op("pe", MM(bank[b][:, qbl * 65:(qbl + 1) * 65], PT[ptp][:, c0:c0 + 128],
                                            V[:, slot, hv, :], first, last),
                                   reads=[f"pt{ptp}_{ki}", f"V{slot}{vsuf}"], writes=[f"bank{b}"])
                            ems.append(em)
                    ems.append(lambda: pv_epilogue(h, b, ycol))
                    return ems

                def pv_epilogue(h, b, ycol):
                    ov = bank[b][:, 0:260].rearrange("p (q d) -> p q d", d=65)
                    lsum = ov[:, :, 64:65].rearrange("p q o -> p (q o)")
                    rli = rl[h % 2]
                    rres = f"rl{h % 2}"
                    if mixer == "A":
                        op("dve", TS(rli[:], lsum, esink[:, h:h + 1], None, ALU.add),
                           reads=[f"bank{b}", "esink"], writes=[rres])
                        op("dve", RCP(rli[:], rli[:]), reads=[rres], writes=[rres])
                    else:
                        op("dve", RCP(rli[:], lsum), reads=[f"bank{b}"], writes=[rres])
                    op("dve", TT(ybuf[:, :, ycol:ycol + 64], ov[:, :, 0:64],
                                 rli[:].unsqueeze(2).to_broadcast([128, 4, 64]), ALU.mult),
                       reads=[f"bank{b}", rres], writes=[f"y{mixer}{h}"])

                yield from qk(0)
                for h in range(1, 8):
                    yield from qk(h, pv(h - 1))
                ems = pv(7)
                for f in ems:
                    f()
                yield 100 * len(ems)

            P.stage = f"CA{g}"
            yield from attention("A")
            P.stage = f"QB{g}"
            for m in range(4):
                proj_fm(C_QB + m * 128, evac_q(m, 4 + m))
                yield 4096
            P.stage = f"CB{g}"
            yield from attention("B")

            P.stage = f"D{g}"
            yslots = {}
            hslots = {}

            def preY(tb):
                xs = xn_slot()
                yslots[tb] = xs
                for m in range(2):
                    mx = "AB"[m]
                    op("act", ACTF(xn[xs][:, m * 512:(m + 1) * 512], ybuf[:, tb, m * 512:(m + 1) * 512],
                                   AF.Square, accum_out=sg[:, 8 + 2 * tb + m:9 + 2 * tb + m]),
                       reads=[f"y{mx}{h}" for h in range(8)], writes=[f"xn{xs}_{m}", f"st{par}_c{tb}"])
                rstd_from_ssq(sg[:, 8 + 2 * tb:10 + 2 * tb], sg[:, 16 + 2 * tb:18 + 2 * tb], 512,
                              f"st{par}_c{tb}", f"st{par}_d{tb}")
                for m in range(2):
                    mx = "AB"[m]
                    op("pool", TS(xn[xs][:, m * 512:(m + 1) * 512], ybuf[:, tb, m * 512:(m + 1) * 512],
                                  sg[:, 16 + 2 * tb + m:17 + 2 * tb + m], 1.0, ALU.mult, ALU.mult),
                       reads=[f"y{mx}{h}" for h in range(8)] + [f"st{par}_d{tb}"], writes=[f"xn{xs}_{m}"])

            def TY(tb):
                xs = yslots[tb]
                transpose_block(xn[xs], XN(xs), nT, "nT", tb, 8)
                xn_busy.discard(xs)

            def OP(tb):
                assert (g, tb) in xloaded, ("x for the residual not loaded yet", g, tb)
                for dmh in range(2):
                    b = next_bankA()
                    for k in range(8):
                        op("pe", MM(bank[b][:], nT[:, k, tb * 128:(tb + 1) * 128],
                                    wout_sb[:, k, dmh * 512:(dmh + 1) * 512], k == 0, k == 7),
                           reads=[f"nT{k}_{tb}", f"wout{k}"], writes=[f"bank{b}"])
                    hs = xg[:, tb, dmh * 512:(dmh + 1) * 512]
                    op("dve", TT(hs, bank[b][:], hs, ALU.add),
                       reads=[f"bank{b}", f"xh{par}_{tb}_{dmh}"], writes=[f"xh{par}_{tb}_{dmh}"])

            def preH(tb):
                xs = xn_slot()
                hslots[tb] = xs
                op("act", ACTF(xn[xs][:], xg[:, tb, :], AF.Square, accum_out=sg[:, 24 + tb:25 + tb]),
                   reads=XH(par, tb), writes=XN(xs) + [f"st{par}_e{tb}"])
                rstd_from_ssq(sg[:, 24 + tb:25 + tb], sg[:, 28 + tb:29 + tb], D, f"st{par}_e{tb}", f"st{par}_f{tb}")
                if tb % 2 == 0:
                    op("act", ACTF(xn[xs][:], xg[:, tb, :], AF.Copy, scale=sg[:, 28 + tb:29 + tb]),
                       reads=XH(par, tb) + [f"st{par}_f{tb}"], writes=XN(xs))
                else:
                    op("pool", TS(xn[xs][:], xg[:, tb, :], sg[:, 28 + tb:29 + tb], 1.0, ALU.mult, ALU.mult),
                       reads=XH(par, tb) + [f"st{par}_f{tb}"], writes=XN(xs))

            def TH(tb):
                n2T_gen["g"] = g
                xs = hslots[tb]
                transpose_block(xn[xs], XN(xs), n2T, "n2T", tb, 16)
                xn_busy.discard(xs)

            preY(0)
            preY(1)
            yield 3000
            yield 3000
            TY(0)
            preY(2)
            yield 2048
            TY(1)
            yield 2048
            OP(0)
            preY(3)
            yield 8192
            TY(2)
            yield 2048
            OP(1)
            preH(0)
            yield 8192
            TY(3)
            yield 2048
            OP(2)
            preH(1)
            yield 8192
            TH(0)
            yield 2048
            OP(3)
            preH(2)
            yield 8192
            TH(1)
            preH(3)
            yield 2048
            yield 3000
            TH(2)
            yield 2048
            TH(3)
            yield 2048

        def P2(g):
            assert totals is None or n2T_gen["g"] == g, ("P2 started before its n2T was complete", g, n2T_gen)
            par = g % 2
            xg = xh[par]
            sg = stat[par]
            t0 = g * 512
            sbase = g * NST
            fcnt = 0
            head = {"n": 0}

            def head_hook():
                head["n"] += 1
                keep = P.stage
                P.stage = f"FN{g - 1}"
                if head["n"] == 1 and g > 0:
                    FN_b(g - 1)
                if 2 <= head["n"] <= 5 and g > 0:
                    FN_c(g - 1, [head["n"] - 2])
                if head["n"] == 7:
                    if g > 0:
                        FN_d(g - 1)
                    if g + 1 < n_groups:
                        x_loads(g + 1)
                    xready[g + 1] = True
                P.stage = keep

            for ffh in range(2):
                P.stage = f"F1_{g}"
                for j in range(8):
                    si = sbase + ffh * 16 + j
                    stream_ensure(si + 4)
                    slot = si % 4
                    w1v = wst[slot][:].rearrange("p (k c) -> p k c", c=256)
                    for cc in range(2):
                        c = j * 2 + cc
                        if ffh == 0:
                            head_hook()
                        b = PB[fcnt % 4]
                        rs = fcnt % 2
                        fcnt += 1
                        assert totals is None or n2T_gen["g"] == g, ("n2T overwritten before FFN1 read it", g, n2T_gen)
                        for k in range(8):
                            op("pe", MM(bank[b][:], w1v[:, k, cc * 128:(cc + 1) * 128], n2T[:, k, :], k == 0, k == 7),
                               reads=NTR("n2T", [k], ALLTB) + [f"wst{slot}"], writes=[f"bank{b}"])
                            if k < 7:
                                yield 512
                        op("dve", TS(rtmp[rs][:], bank[b][:], 0.0, None, ALU.max), reads=[f"bank{b}"], writes=[f"rtmp{rs}"])
                        op("pool", TT(uT[:, c, :], rtmp[rs][:], rtmp[rs][:], ALU.mult),
                           reads=[f"rtmp{rs}"], writes=[f"uT{c}"])
                        yield 512
                P.stage = f"F2_{g}"
                for dmh in range(2):
                    for q in range(4):
                        si = sbase + ffh * 16 + 8 + dmh * 4 + q
                        stream_ensure(si + 4)
                        slot = si % 4
                        w2v = wst[slot][:].rearrange("p (r c) -> p r c", c=512)
                        for tb in range(4):
                            b = PB[tb]
                            for r in range(4):
                                c = q * 4 + r
                                op("pe", MM(bank[b][:], uT[:, c, tb * 128:(tb + 1) * 128], w2v[:, r, :],
                                            q == 0 and r == 0, q == 3 and r == 3),
                                   reads=[f"uT{c}", f"wst{slot}"], writes=[f"bank{b}"])
                                if r < 3:
                                    yield 512
                            if q == 3:
                                hs = xg[:, tb, dmh * 512:(dmh + 1) * 512]
                                op("dve", TT(hs, bank[b][:], hs, ALU.add),
                                   reads=[f"bank{b}", f"xh{par}_{tb}_{dmh}"], writes=[f"xh{par}_{tb}_{dmh}"])
                            yield 512
            P.stage = f"FN{g}"
            FN_a(g)
            yield 0

        def drain(gen):
            for _ in gen:
                pass

        xready = {}
        W = {"P1": 228000.0, "P2": 262144.0}
        meas = {}

        def drain(gen, key=None):
            for c in gen:
                if key:
                    meas[key] += c

        LAG = 1.17

        meas["P1"] = []
        meas["P2"] = []
        pos = {"P1": 0.0, "P2": 0.0}

        def chain(fn, key):
            for g in range(n_groups):
                tot = 0.0
                wg = (totals[key][g] if totals else W[key])
                for c in fn(g):
                    tot += c
                    pos[key] = g + min(tot / wg, 1.0)
                    yield c
                pos[key] = g + 1.0
                meas[key].append(tot)

        def run_pipeline(c1, c2):
            s1 = s2 = ""
            d1 = d2 = False
            while not (d1 and d2):
                take2 = (not d2) and (d1 or (pos["P2"] + LAG <= pos["P1"]))
                if take2:
                    P.stage = s2
                    try:
                        next(c2)
                    except StopIteration:
                        d2 = True
                    s2 = P.stage
                else:
                    P.stage = s1
                    try:
                        next(c1)
                    except StopIteration:
                        d1 = True
                    s1 = P.stage

        P.stage = "setup"
        setup_early()
        c1 = chain(P1, "P1")
        c2 = chain(P2, "P2")
        for _ in range(7):
            next(c1)
        P.stage = "setup"
        setup_mid()
        for _ in range(5):
            next(c1)
        P.stage = "setup"
        setup_late()
        setup_tb()
        stream_ensure(4)
        run_pipeline(c1, c2)
        P.stage = "FNlast"
        FN_b(n_groups - 1)
        FN_c(n_groups - 1)
        FN_d(n_groups - 1)
        fin = op("sp", None)
        fin.deps = list(out_dmas)
        P.emit(st)
        nc._prog_stats = P.stats
        nc._totals = dict(meas)
        nc._prog_tags = {e: [o.tag for o in P.ops[e] if o.fn is not None] for e in ENGS}
    return nc


def _consts():
    s = np.arange(128)[:, None]
    u = np.arange(256)[None, :]
    sc, uc = s // 64, u // 64
    validA = (sc <= uc) & (uc <= sc + 2)
    validB = (sc <= uc)
    slopes = 2.0 ** (-(np.arange(8) + 1.0))
    ta = np.where(validA[None], -slopes[:, None, None] * np.abs(u - s)[None].astype(np.float64), NEG)
    cst = np.zeros((128, 704), np.float32)
    cst[:, 0:128] = np.eye(128, dtype=np.float32)
    cst[:, 128:384] = validB.astype(np.float32)
    cst[:, 384:640] = np.where(validB, 0.0, NEG)
    cst[0:64, 640:704] = NEG
    idx = np.clip(u - s, -128, 128) + 128
    return ta.astype(np.float32), cst, idx


def kernel(x, norm1_g, w_in, sinks_a, rel_bias_b, out_norm_a_g, out_norm_b_g,
           w_out, norm2_g, w_ff1, w_ff2, final_norm_g):
    x = np.asarray(x, np.float32)
    B = x.shape[0]
    ta, cst, idx = _consts()
    w_in0 = np.asarray(w_in, np.float32)[0]
    perm = []
    for j in range(4):
        perm += list(range(j * 64, (j + 1) * 64)) + list(range((4 + j) * 64, (5 + j) * 64))
    cols = np.concatenate([np.array(perm), np.arange(512, PROJ)])
    w_in_p = np.ascontiguousarray(w_in0[:, cols])
    gcat = np.concatenate([np.asarray(norm1_g, np.float32)[0],
                           np.asarray(out_norm_a_g, np.float32)[0], np.asarray(out_norm_b_g, np.float32)[0],
                           np.asarray(norm2_g, np.float32)[0]])
    gT = np.ascontiguousarray(gcat.reshape(24, 128).T)
    tbraw = np.ascontiguousarray(np.asarray(rel_bias_b, np.float32)[0][:, idx])
    shared = {
        "w_in": w_in_p,
        "w_out": np.ascontiguousarray(np.asarray(w_out, np.float32)[0]),
        "w_ff1": np.ascontiguousarray(np.asarray(w_ff1, np.float32)[0]),
        "w_ff2": np.ascontiguousarray(np.asarray(w_ff2, np.float32)[0]),
        "gT": gT,
        "gf": np.asarray(final_norm_g, np.float32).reshape(1, D),
        "sinks": np.asarray(sinks_a, np.float32).reshape(1, 8),
        "tbraw": tbraw,
        "ta": ta,
        "cst": cst,
    }
    nc = build_nc(totals=build_nc()._totals)
    in_maps = [dict(shared, x=np.ascontiguousarray(x[b])) for b in range(B)]
    res = run_bass_kernel_spmd(nc, in_maps, core_ids=list(range(B)))
    return np.stack([np.asarray(r["out"], np.float32) for r in res.results], axis=0)
```

```python
from contextlib import ExitStack

import numpy as np
import concourse.bass as bass
import concourse.mybir as mybir
from concourse.bass_utils import run_bass_kernel_spmd

F32 = mybir.dt.float32
BF16 = mybir.dt.bfloat16
AF = mybir.ActivationFunctionType
ALU = mybir.AluOpType


def MM(out, lhsT, rhs, start, stop):
    return lambda e: e.matmul(out=out, lhsT=lhsT, rhs=rhs, start=start, stop=stop)


def TR(out, in_, identity):
    return lambda e: e.transpose(out=out, in_=in_, identity=identity)


def ACTF(out, in_, func, **kw):
    return lambda e: e.activation(out=out, in_=in_, func=func, **kw)


def TS(out, in0, scalar1, scalar2, op0, op1=None):
    if op1 is None:
        return lambda e: e.tensor_scalar(out=out, in0=in0, scalar1=scalar1, scalar2=scalar2, op0=op0)
    return lambda e: e.tensor_scalar(out=out, in0=in0, scalar1=scalar1, scalar2=scalar2, op0=op0, op1=op1)


def TT(out, in0, in1, op):
    return lambda e: e.tensor_tensor(out=out, in0=in0, in1=in1, op=op)


def STT(out, in0, scalar, in1, op0, op1):
    return lambda e: e.scalar_tensor_tensor(out=out, in0=in0, scalar=scalar, in1=in1, op0=op0, op1=op1)


def CP(out, in_):
    return lambda e: e.tensor_copy(out=out, in_=in_)


def RCP(out, in_):
    return lambda e: e.reciprocal(out=out, in_=in_)


def DMA(out, in_):
    return lambda e: e.dma_start(out=out, in_=in_)


def MS(ap, val):
    return lambda e: e.memset(ap, val)


S = 4096
D = 1024
NG = 8
DFF = 4096
PROJ = 2304
EPS = 1e-6
NEG = -30000.0
C_QA, C_KA, C_VA, C_QB, C_KB, C_VB = 0, 512, 640, 768, 1280, 1792

ENGS = ("pe", "act", "dve", "pool", "sp")
EPOCH = 16000


class Res:
    __slots__ = ("name", "last_w", "readers")

    def __init__(self, name):
        self.name = name
        self.last_w = None
        self.readers = []


class Op:
    __slots__ = ("eng", "fn", "deps", "sig", "dma_key", "dma_val", "has_dep", "tag")

    def __init__(self, eng, fn):
        self.eng = eng
        self.fn = fn
        self.deps = []
        self.sig = None
        self.dma_key = None
        self.dma_val = None
        self.has_dep = False


class Prog:
    def __init__(self, nc):
        self.nc = nc
        self.ops = {e: [] for e in ENGS}
        self.dma_cnt = {}
        self.res = {}
        self.stage = ""

    def R(self, name):
        r = self.res.get(name)
        if r is None:
            r = Res(name)
            self.res[name] = r
        return r

    def _rs(self, lst):
        out = []
        for x in lst:
            if x is None:
                continue
            if isinstance(x, str):
                out.append(self.R(x))
            else:
                out.extend(self._rs(x))
        return out

    def op(self, eng, fn, reads=(), writes=(), dma=None):
        o = Op(eng, fn)
        o.tag = self.stage
        reads = self._rs(reads)
        writes = self._rs(writes)
        deps = {}

        def add(p, kind):
            if p is None or p is o:
                return
            if p.dma_key is None and p.eng == eng and eng == "pe":
                return
            deps[id(p)] = p

        for r in reads:
            add(r.last_w, "raw")
        for w in writes:
            add(w.last_w, "waw")
            for rd in w.readers:
                add(rd, "war")
        for r in reads:
            if dma is None:
                r.readers = [q for q in r.readers if q.dma_key is not None or q.eng != eng]
            r.readers.append(o)
        for w in writes:
            w.last_w = o
            w.readers = []
        o.deps = list(deps.values())
        for p in o.deps:
            p.has_dep = True
        if dma is not None:
            c = self.dma_cnt.get(dma, 0) + 16
            self.dma_cnt[dma] = c
            o.dma_key = dma
            o.dma_val = c
        self.ops[eng].append(o)
        return o

    def emit(self, stack):
        nc = self.nc
        nsig = {}
        for e in ENGS:
            k = 0
            for o in self.ops[e]:
                if o.dma_key is None and o.has_dep:
                    k += 1
                    o.sig = k
            nsig[e] = k
        esem = {}
        for e in ENGS:
            ne = (nsig[e] + EPOCH - 1) // EPOCH
            esem[e] = [stack.enter_context(nc.semaphore(f"s_{e}{i}")) for i in range(ne)]
        dsem = {k: stack.enter_context(nc.semaphore(f"d_{k}")) for k in self.dma_cnt}
        self.stats = {e: [len(self.ops[e]), nsig[e], 0] for e in ENGS}
        block = stack.enter_context(nc.Block())

        def run(e, engine):
            waited = {}
            for o in self.ops[e]:
                need = {}
                for p in o.deps:
                    if p.dma_key is not None:
                        key = ("d", p.dma_key)
                        val = p.dma_val
                    else:
                        key = ("e", p.eng)
                        val = p.sig
                    if need.get(key, 0) < val:
                        need[key] = val
                for key, val in need.items():
                    if waited.get(key, 0) >= val:
                        continue
                    waited[key] = val
                    if key[0] == "d":
                        sem = dsem[key[1]]
                        wv = val
                    else:
                        ep = (val - 1) // EPOCH
                        sem = esem[key[1]][ep]
                        wv = val - ep * EPOCH
                    engine.wait_ge(sem, wv)
                    self.stats[e][2] += 1
                if o.fn is None:
                    continue
                ins = o.fn(engine)
                if o.dma_key is not None:
                    ins.then_inc(dsem[o.dma_key], 16)
                elif o.sig is not None:
                    ep = (o.sig - 1) // EPOCH
                    ins.then_inc(esem[e][ep], 1)

        @block.tensor
        def _(eng):
            run("pe", eng)

        @block.scalar
        def _(eng):
            run("act", eng)

        @block.vector
        def _(eng):
            run("dve", eng)

        @block.gpsimd
        def _(eng):
            run("pool", eng)

        @block.sync
        def _(eng):
            run("sp", eng)


def build_nc(n_groups=NG, debug=False, totals=None):
    nc = bass.Bass("TRN2", target_bir_lowering=False)
    dram = lambda n, s, d=F32, kind="ExternalInput": nc.dram_tensor(n, s, d, kind=kind).ap()
    x_d = dram("x", [S, D])
    win_d = dram("w_in", [D, PROJ])
    wout_d = dram("w_out", [D, D])
    w1_d = dram("w_ff1", [D, DFF])
    w2_d = dram("w_ff2", [DFF, D])
    gT_d = dram("gT", [128, 24])
    gf_d = dram("gf", [1, D])
    sinks_d = dram("sinks", [1, 8])
    tbraw_d = dram("tbraw", [8, 128, 256])
    ta_d = dram("ta", [8, 128, 256])
    cst_d = dram("cst", [128, 128 + 256 + 256 + 64])
    out_d = dram("out", [S, D], F32, kind="ExternalOutput")
    w1bf_d = dram("w1bf", [16, 128, 8, 256], BF16, kind="Internal")
    w2bf_d = dram("w2bf", [16, 128, 4, 512], BF16, kind="Internal")
    dbg = {}
    if debug:
        dbg["qp"] = dram("dbg_qp", [8, 128, 512], F32, kind="ExternalOutput")
        dbg["kt"] = dram("dbg_kt", [128, 5, 1024], F32, kind="ExternalOutput")
        dbg["v"] = dram("dbg_v", [128, 8 * 10 * 65], F32, kind="ExternalOutput")
        dbg["y"] = dram("dbg_y", [128, 4 * 1024], F32, kind="ExternalOutput")
        dbg["h"] = dram("dbg_h", [128, 4 * 1024], F32, kind="ExternalOutput")

    with ExitStack() as st:
        sb = lambda n, s, d: st.enter_context(nc.sbuf_tensor(n, s, d))
        win_sb = sb("win_sb", [128, 8, PROJ], BF16)
        wout_sb = sb("wout_sb", [128, 8, D], BF16)
        wst = [sb(f"wst{i}", [128, 2048], BF16) for i in range(4)]
        xh = [sb(f"xh{i}", [128, 4, D], F32) for i in range(2)]
        xn = [sb(f"xn{i}", [128, D], BF16) for i in range(2)]
        xtmp = [sb(f"xtmp{i}", [128, D], F32) for i in range(2)]
        nT = sb("nT", [128, 8, 512], BF16)
        n2T = sb("n2T", [128, 8, 512], BF16)
        QP = [sb(f"qp{i}", [128, 512], BF16) for i in range(8)]
        KT = sb("KT", [128, 5, 1024], BF16)
        V = sb("V", [128, 8, 10, 65], BF16)
        PT = [sb(f"pt{i}", [128, 2560], BF16) for i in range(2)]
        ybuf = sb("ybuf", [128, 4, D], BF16)
        uT = sb("uT", [128, 16, 512], BF16)
        rtmp = [sb(f"rtmp{i}", [128, 512], BF16) for i in range(2)]
        TA = sb("TA", [128, 8, 256], BF16)
        TB = sb("TB", [128, 8, 256], BF16)
        ident = sb("ident", [128, 128], BF16)
        mask9 = sb("mask9", [128, 64], BF16)
        gf = sb("gf_sb", [128, D], F32)
        gT = sb("gT_sb", [128, 24], F32)
        esink = sb("esink", [128, 8], F32)
        mhalf = sb("mhalf", [128, 8], F32)
        stat = [sb(f"stat{i}", [128, 40], F32) for i in range(2)]
        rl = [sb(f"rl{i}", [128, 4], F32) for i in range(2)]
        bank = [st.enter_context(nc.psum_tensor(f"bank{i}", [128, 512], F32)) for i in range(8)]
        PA = [0, 1, 2, 3]
        PB = [4, 5, 6, 7]

        P = Prog(nc)
        op = P.op

        WBLK = {"KAV": (512, 768), "KB": (1280, 1792), "VB": (1792, 2304), "QA": (0, 512), "QB": (768, 1280)}

        def win_load(name):
            c0, c1 = WBLK[name]
            op("pool", DMA(win_sb[:, :, c0:c1], win_d[:, c0:c1].rearrange("(k p) c -> p k c", p=128)),
               writes=[f"win_{name}"], dma=f"c_win_{name}")

        def win_res(col0):
            for name, (c0, c1) in WBLK.items():
                if c0 <= col0 < c1:
                    return f"win_{name}"
            raise KeyError(col0)

        tbtmp = uT[:].rearrange("p a b -> p (a b)").bitcast(F32)
        UT_ALL = [f"uT{c}" for c in range(16)]
        tbraw = tbtmp[:, 0:2048].rearrange("p (h u) -> p h u", u=256)
        vneg = tbtmp[:, 2048:2560]

        def setup_early():
            op("sp", DMA(gT[:], gT_d), writes=["gT"], dma="c_gT")
            op("pool", MS(mhalf[:], -0.5), writes=["mhalf"])
            op("pool", DMA(ident[:], cst_d[:, 0:128]), writes=["ident"], dma="c_id")
            win_load("KAV")
            win_load("KB")

        def setup_mid():
            x_loads(0)
            win_load("VB")
            win_load("QA")
            op("pool", MS(V[:].rearrange("p a b c -> p (a b c)"), 1.0),
               writes=[f"V{i}{sfx}" for i in range(8) for sfx in ("_a", "_b")])
            for i in range(8):
                op("pool", MS(QP[i][:], 0.0), writes=[f"qp{i}"])
            op("pool", DMA(TA[:], ta_d.rearrange("h p u -> p h u")), writes=["TA"], dma="c_ta")
            op("pool", DMA(mask9[:], cst_d[:, 640:704]), writes=["mask9"], dma="c_m9")
            win_load("QB")
            op("sp", DMA(esink[:], sinks_d.partition_broadcast(128)), writes=["esink"], dma="c_sink")
            op("sp", DMA(gf[:], gf_d.partition_broadcast(128)), writes=["gf"], dma="c_gf")
            op("sp", DMA(tbraw, tbraw_d.rearrange("h p u -> p h u")), writes=UT_ALL, dma="c_tb")
            op("sp", DMA(vneg, cst_d[:, 128:640]), writes=["vneg"], dma="c_vn")

        def setup_tb():
            for h in range(8):
                op("dve", TS(tbraw[:, h, :], tbraw[:, h, :], tbraw[:, h, 255:256], None, ALU.subtract),
                   reads=UT_ALL, writes=UT_ALL)
                op("dve", TT(tbraw[:, h, :], tbraw[:, h, :], vneg[:, 0:256], ALU.mult),
                   reads=UT_ALL + ["vneg"], writes=UT_ALL)
                op("dve", TT(TB[:, h, :], tbraw[:, h, :], vneg[:, 256:512], ALU.add),
                   reads=UT_ALL + ["vneg"], writes=["TB"])
            op("act", ACTF(esink[:], esink[:], AF.Exp), reads=["esink"], writes=["esink"])

        def setup_late():
            for k in range(8):
                op("pool", DMA(wout_sb[:, k, :], wout_d[k * 128:(k + 1) * 128, :]), writes=[f"wout{k}"], dma=f"c_wout{k}")

        stream = []
        for ffh in range(2):
            for j in range(8):
                stream.append(("w1", ffh * 8 + j))
            for dmh in range(2):
                for q in range(4):
                    stream.append(("w2", (ffh * 2 + dmh) * 4 + q))
        NST = len(stream)
        st_state = {"next": 0}

        def stream_ensure(upto):
            upto = min(upto, NST * n_groups)
            while st_state["next"] < upto:
                s_ = st_state["next"]
                kind, idx = stream[s_ % NST]
                slot = s_ % 4
                if kind == "w1":
                    scr = w1bf_d[idx].rearrange("p k c -> p (k c)")
                    rd = f"w1bf{idx}"
                    src32 = w1_d[:, idx * 256:(idx + 1) * 256].rearrange("(k p) c -> p k c", p=128)
                    dst3 = wst[slot][:].rearrange("p (k c) -> p k c", c=256)
                else:
                    scr = w2bf_d[idx].rearrange("p r c -> p (r c)")
                    rd = f"w2bf{idx}"
                    ffh, dmh, q = idx // 8, (idx // 4) % 2, idx % 4
                    r0 = (ffh * 16 + q * 4) * 128
                    src32 = w2_d[r0:r0 + 512, dmh * 512:(dmh + 1) * 512].rearrange("(r p) c -> p r c", p=128)
                    dst3 = wst[slot][:].rearrange("p (r c) -> p r c", c=512)
                if s_ < NST:
                    op("pool", DMA(dst3, src32), writes=[f"wst{slot}"], dma=f"wstc{slot}")
                    if n_groups > 1:
                        op("sp", DMA(scr, wst[slot][:]), reads=[f"wst{slot}"], writes=[rd], dma=f"wb{slot}")
                else:
                    op("sp", DMA(wst[slot][:], scr), reads=[rd], writes=[f"wst{slot}"], dma=f"wst{slot}")
                st_state["next"] += 1

        rot = {"A": 0, "xn": 0}
        NXN = len(xn)

        def next_bankA():
            b = PA[rot["A"] % 4]
            rot["A"] += 1
            return b

        xn_busy = set()

        def xn_slot():
            for _ in range(NXN):
                s_ = rot["xn"] % NXN
                rot["xn"] += 1
                if s_ not in xn_busy:
                    xn_busy.add(s_)
                    return s_
            raise AssertionError("no free xn slot")

        def XN(s_):
            return [f"xn{s_}_0", f"xn{s_}_1"]

        def XH(par, tb):
            return [f"xh{par}_{tb}_0", f"xh{par}_{tb}_1"]

        def NTR(name, ks, tbs):
            return [f"{name}{k}_{t}" for k in ks for t in tbs]

        def rstd_from_ssq(sq_ap, r_ap, n, res_sq, res_r):
            k = sq_ap.shape[1]
            op("pool", TS(sq_ap, sq_ap, 1.0 / n, EPS, ALU.mult, ALU.add), reads=[res_sq], writes=[res_sq])
            op("pool", TT(r_ap, sq_ap, mhalf[:, 0:k], ALU.pow), reads=[res_sq, "mhalf"], writes=[res_r])

        def transpose_block(src_bf, src_res, dst, dname, tb, gcol):
            b = PA[rot["A"] % 2]
            rot["A"] += 1
            pb = bank[b][:].bitcast(BF16)
            for k in range(8):
                op("pe", TR(pb[:, k * 128:(k + 1) * 128], src_bf[:, k * 128:(k + 1) * 128], ident[:]),
                   reads=[src_res, "ident"], writes=[f"bank{b}"])
            op("dve", TT(dst[:, :, tb * 128:(tb + 1) * 128], pb.rearrange("p (k t) -> p k t", t=128),
                         gT[:, gcol:gcol + 8].unsqueeze(2).to_broadcast([128, 8, 128]), ALU.mult),
               reads=[f"bank{b}", "gT"], writes=[f"{dname}{k}_{tb}" for k in range(8)])

        out_dmas = []
        ALLTB = range(4)

        xloaded = set()
        n2T_gen = {"g": -1}

        def x_loads(g):
            par = g % 2
            for tb in range(4):
                xloaded.add((g, tb))
                op("sp", DMA(xh[par][:, tb, :], x_d[g * 512 + tb * 128:g * 512 + (tb + 1) * 128, :]),
                   writes=XH(par, tb), dma=f"x{par}{tb}")

        def FN_a(g):
            par = g % 2
            sg = stat[par]
            junk = uT[:, 0:2, :].rearrange("p a b -> p (a b)")
            for tb in range(4):
                op("act", ACTF(junk, xh[par][:, tb, :], AF.Square, accum_out=sg[:, 32 + tb:33 + tb]),
                   reads=XH(par, tb), writes=["uT0", "uT1", f"st{par}_g{tb}"])

        def FN_b(g):
            par = g % 2
            sg = stat[par]
            for tb in range(4):
                rstd_from_ssq(sg[:, 32 + tb:33 + tb], sg[:, 36 + tb:37 + tb], D, f"st{par}_g{tb}", f"st{par}_h{tb}")

        def FN_c(g, tbs=None):
            par = g % 2
            sg = stat[par]
            xg = xh[par]
            t0 = g * 512
            for tb in (range(4) if tbs is None else tbs):
                op("dve", STT(xg[:, tb, :], xg[:, tb, :], sg[:, 36 + tb:37 + tb], gf[:], ALU.mult, ALU.mult),
                   reads=XH(par, tb) + [f"st{par}_h{tb}", "gf"], writes=XH(par, tb))

        def FN_d(g):
            par = g % 2
            xg = xh[par]
            t0 = g * 512
            for tb in range(4):
                o = op("sp", DMA(out_d[t0 + tb * 128:t0 + (tb + 1) * 128, :], xg[:, tb, :]),
                       reads=XH(par, tb), dma=f"o{par}{tb}")
                out_dmas.append(o)

        def P1(g):
            par = g % 2
            xg = xh[par]
            sg = stat[par]
            t0 = g * 512
            P.stage = f"A{g}"
            slots = {}

            def xt_load(tb):
                xt = tb % 2
                q_ = "sp" if tb < 2 else "pool"
                op(q_, DMA(xtmp[xt][:], x_d[t0 + tb * 128:t0 + (tb + 1) * 128, :]),
                   writes=[f"xtmp{xt}"], dma=f"xt{xt}_{q_}")

            def preA(tb):
                xs = xn_slot()
                slots[tb] = xs
                xt = tb % 2
                if tb < 2:
                    xt_load(tb)
                op("act", ACTF(xn[xs][:], xtmp[xt][:], AF.Square, accum_out=sg[:, tb:tb + 1]),
                   reads=[f"xtmp{xt}"], writes=XN(xs) + [f"st{par}_a{tb}"])
                rstd_from_ssq(sg[:, tb:tb + 1], sg[:, 4 + tb:5 + tb], D, f"st{par}_a{tb}", f"st{par}_b{tb}")
                if tb % 2 == 0:
                    op("act", ACTF(xn[xs][:], xtmp[xt][:], AF.Copy, scale=sg[:, 4 + tb:5 + tb]),
                       reads=[f"xtmp{xt}", f"st{par}_b{tb}"], writes=XN(xs))
                else:
                    op("pool", TS(xn[xs][:], xtmp[xt][:], sg[:, 4 + tb:5 + tb], 1.0, ALU.mult, ALU.mult),
                       reads=[f"xtmp{xt}", f"st{par}_b{tb}"], writes=XN(xs))
                if tb + 2 < 4:
                    xt_load(tb + 2)

            def TA_(tb):
                xs = slots[tb]
                transpose_block(xn[xs], XN(xs), nT, "nT", tb, 0)
                xn_busy.discard(xs)

            preA(0)
            preA(1)
            for _ in range(3):
                yield 6000
            TA_(0)
            preA(2)
            yield 5000
            TA_(1)
            preA(3)
            yield 5000
            TA_(2)
            yield 4000
            TA_(3)
            yield 2048

            P.stage = f"B{g}"

            def proj_fm(col0, evac):
                b = next_bankA()
                for k in range(8):
                    op("pe", MM(bank[b][:], win_sb[:, k, col0:col0 + 128], nT[:, k, :], k == 0, k == 7),
                       reads=NTR("nT", [k], ALLTB) + [win_res(col0)], writes=[f"bank{b}"])
                evac(b)

            def evac_k(chunk):
                def f(b):
                    op("act", ACTF(KT[:, chunk, par * 512:(par + 1) * 512], bank[b][:], AF.Copy),
                       reads=[f"bank{b}"], writes=[f"KT{par}_{chunk}"])
                return f

            def evac_q(t_lo, t_hi):
                def f(b):
                    op("act", ACTF(QP[t_lo][0:64, :], bank[b][0:64, :], AF.Copy, scale=0.125),
                       reads=[f"bank{b}"], writes=[f"qp{t_lo}"])
                    op("act", ACTF(QP[t_hi][64:128, :], bank[b][64:128, :], AF.Copy, scale=0.125),
                       reads=[f"bank{b}"], writes=[f"qp{t_hi}"])
                return f

            proj_fm(C_KA, evac_k(0))
            yield 4096
            for m in range(4):
                proj_fm(C_KB + m * 128, evac_k(1 + m))
                yield 4096
            for tb in range(4):
                slot = (g * 4 + tb) % 8
                b1 = next_bankA()
                b2 = next_bankA()
                for k in range(8):
                    op("pe", MM(bank[b1][:, 0:128], nT[:, k, tb * 128:(tb + 1) * 128],
                                win_sb[:, k, C_VA:C_VA + 128], k == 0, k == 7),
                       reads=[f"nT{k}_{tb}", "win_KAV"], writes=[f"bank{b1}"])
                for k in range(8):
                    op("pe", MM(bank[b2][:], nT[:, k, tb * 128:(tb + 1) * 128],
                                win_sb[:, k, C_VB:C_VB + 512], k == 0, k == 7),
                       reads=[f"nT{k}_{tb}", "win_VB"], writes=[f"bank{b2}"])
                op("dve", CP(V[:, slot, 0:2, 0:64], bank[b1][:, 0:128].rearrange("p (h d) -> p h d", d=64)),
                   reads=[f"bank{b1}"], writes=[f"V{slot}_a"])
                op("dve", CP(V[:, slot, 2:10, 0:64], bank[b2][:].rearrange("p (h d) -> p h d", d=64)),
                   reads=[f"bank{b2}"], writes=[f"V{slot}_b"])
                yield 5120
            for j in range(4):
                proj_fm(C_QA + j * 128, evac_q(j, 4 + j))
                yield 4096

            def attention(mixer):
                W = 3 if mixer == "A" else 9
                TX = TA if mixer == "A" else TB
                txres = "TA" if mixer == "A" else "TB"
                vsuf = "_a" if mixer == "A" else "_b"
                back = 1 if mixer == "A" else 4
                kbs = [kb for kb in range(4 * g - back, 4 * g + 4) if kb >= 0]
                info = {}
                off = 0
                for ki, kb in enumerate(kbs):
                    cs = max(2 * kb, 8 * g)
                    ce = min(2 * kb + W, 8 * g + 7)
                    u0 = 64 * (cs - 2 * kb)
                    u1 = 64 * (ce + 1 - 2 * kb)
                    info[kb] = (u0, u1, off, 64 * (cs - 8 * g), ki)
                    off += u1 - u0
                sbank = [PA[0], PA[1]]
                obank = [PA[2], PA[3]]
                cnt = {"s": 0}

                def qk(h, fill=()):
                    fill = list(fill)
                    nkb = len(kbs)
                    ptp = h % 2
                    if mixer == "A":
                        qt = h
                        kchunk = 0
                    else:
                        qt = (h // 2) if h % 2 == 0 else 4 + h // 2
                        kchunk = 1 + h // 2
                    for kidx, kb in enumerate(kbs):
                        u0, u1, poff, qc0, ki = info[kb]
                        n = u1 - u0
                        b = sbank[cnt["s"] % 2]
                        cnt["s"] += 1
                        kcol = (kb % 8) * 128
                        kpar = (kb // 4) % 2
                        mms = [(bank[b][:, 0:n], KT[:, kchunk, kcol:kcol + 128], QP[qt][:, qc0:qc0 + n],
                                [f"KT{kpar}_{kchunk}", f"qp{qt}"])]
                        if u0 < 256:
                            ub1 = min(u1, 256)
                            mms.append((bank[b][:, 0:ub1 - u0], ident[:], TX[:, h, u0:ub1], ["ident", txres]))
                        if mixer == "B" and u1 == 640:
                            mms.append((bank[b][:, 576 - u0:640 - u0], ident[:], mask9[:], ["ident", "mask9"]))
                        cost = 0
                        for i, (o_, l_, r_, rd) in enumerate(mms):
                            op("pe", MM(o_, l_, r_, i == 0, i == len(mms) - 1), reads=rd, writes=[f"bank{b}"])
                            cost += o_.shape[1]
                        op("act", ACTF(PT[ptp][:, poff:poff + n], bank[b][:, 0:n], AF.Exp),
                           reads=[f"bank{b}"], writes=[f"pt{ptp}_{ki}"])
                        left = nkb - kidx
                        take = (len(fill) + left - 1) // left
                        for f in fill[:take]:
                            f()
                        fill = fill[take:]
                        yield cost + 100 * take
                    for f in fill:
                        f()

                def pv(h):
                    ptp = h % 2
                    b = obank[h % 2]
                    hv = (h // 4) if mixer == "A" else 2 + h
                    ycol = (0 if mixer == "A" else 512) + h * 64
                    ems = []
                    for qbl in range(4):
                        qb = 4 * g + qbl
                        ks = [kb for kb in range(qb - back, qb + 1) if kb >= 0]
                        for i, kb in enumerate(ks):
                            u0, u1, poff, qc0, ki = info[kb]
                            c0 = poff + 128 * (qb - kb) - u0
                            slot = kb % 8

                            def em(qbl=qbl, c0=c0, slot=slot, ki=ki, first=(i == 0), last=(i == len(ks) - 1)):
                                op("pe", MM(bank[b][:, qbl * 65:(qbl + 1) * 65], PT[ptp][:, c0:c0 + 128],
                                            V[:, slot, hv, :], first, last),
                                   reads=[f"pt{ptp}_{ki}", f"V{slot}{vsuf}"], writes=[f"bank{b}"])
                            ems.append(em)
                    ems.append(lambda: pv_epilogue(h, b, ycol))
                    return ems

                def pv_epilogue(h, b, ycol):
                    ov = bank[b][:, 0:260].rearrange("p (q d) -> p q d", d=65)
                    lsum = ov[:, :, 64:65].rearrange("p q o -> p (q o)")
                    rli = rl[h % 2]
                    rres = f"rl{h % 2}"
                    if mixer == "A":
                        op("dve", TS(rli[:], lsum, esink[:, h:h + 1], None, ALU.add),
                           reads=[f"bank{b}", "esink"], writes=[rres])
                        op("dve", RCP(rli[:], rli[:]), reads=[rres], writes=[rres])
                    else:
                        op("dve", RCP(rli[:], lsum), reads=[f"bank{b}"], writes=[rres])
                    op("dve", TT(ybuf[:, :, ycol:ycol + 64], ov[:, :, 0:64],
                                 rli[:].unsqueeze(2).to_broadcast([128, 4, 64]), ALU.mult),
                       reads=[f"bank{b}", rres], writes=[f"y{mixer}{h}"])

                yield from qk(0)
                for h in range(1, 8):
                    yield from qk(h, pv(h - 1))
                ems = pv(7)
                for f in ems:
                    f()
                yield 100 * len(ems)

            P.stage = f"CA{g}"
            yield from attention("A")
            P.stage = f"QB{g}"
            for m in range(4):
                proj_fm(C_QB + m * 128, evac_q(m, 4 + m))
                yield 4096
            P.stage = f"CB{g}"
            yield from attention("B")

            P.stage = f"D{g}"
            yslots = {}
            hslots = {}

            def preY(tb):
                xs = xn_slot()
                yslots[tb] = xs
                for m in range(2):
                    mx = "AB"[m]
                    op("act", ACTF(xn[xs][:, m * 512:(m + 1) * 512], ybuf[:, tb, m * 512:(m + 1) * 512],
                                   AF.Square, accum_out=sg[:, 8 + 2 * tb + m:9 + 2 * tb + m]),
                       reads=[f"y{mx}{h}" for h in range(8)], writes=[f"xn{xs}_{m}", f"st{par}_c{tb}"])
                rstd_from_ssq(sg[:, 8 + 2 * tb:10 + 2 * tb], sg[:, 16 + 2 * tb:18 + 2 * tb], 512,
                              f"st{par}_c{tb}", f"st{par}_d{tb}")
                for m in range(2):
                    mx = "AB"[m]
                    op("pool", TS(xn[xs][:, m * 512:(m + 1) * 512], ybuf[:, tb, m * 512:(m + 1) * 512],
                                  sg[:, 16 + 2 * tb + m:17 + 2 * tb + m], 1.0, ALU.mult, ALU.mult),
                       reads=[f"y{mx}{h}" for h in range(8)] + [f"st{par}_d{tb}"], writes=[f"xn{xs}_{m}"])

            def TY(tb):
                xs = yslots[tb]
                transpose_block(xn[xs], XN(xs), nT, "nT", tb, 8)
                xn_busy.discard(xs)

            def OP(tb):
                assert (g, tb) in xloaded, ("x for the residual not loaded yet", g, tb)
                for dmh in range(2):
                    b = next_bankA()
                    for k in range(8):
                        op("pe", MM(bank[b][:], nT[:, k, tb * 128:(tb + 1) * 128],
                                    wout_sb[:, k, dmh * 512:(dmh + 1) * 512], k == 0, k == 7),
                           reads=[f"nT{k}_{tb}", f"wout{k}"], writes=[f"bank{b}"])
                    hs = xg[:, tb, dmh * 512:(dmh + 1) * 512]
                    op("dve", TT(hs, bank[b][:], hs, ALU.add),
                       reads=[f"bank{b}", f"xh{par}_{tb}_{dmh}"], writes=[f"xh{par}_{tb}_{dmh}"])

            def preH(tb):
                xs = xn_slot()
                hslots[tb] = xs
                op("act", ACTF(xn[xs][:], xg[:, tb, :], AF.Square, accum_out=sg[:, 24 + tb:25 + tb]),
                   reads=XH(par, tb), writes=XN(xs) + [f"st{par}_e{tb}"])
                rstd_from_ssq(sg[:, 24 + tb:25 + tb], sg[:, 28 + tb:29 + tb], D, f"st{par}_e{tb}", f"st{par}_f{tb}")
                if tb % 2 == 0:
                    op("act", ACTF(xn[xs][:], xg[:, tb, :], AF.Copy, scale=sg[:, 28 + tb:29 + tb]),
                       reads=XH(par, tb) + [f"st{par}_f{tb}"], writes=XN(xs))
                else:
                    op("pool", TS(xn[xs][:], xg[:, tb, :], sg[:, 28 + tb:29 + tb], 1.0, ALU.mult, ALU.mult),
                       reads=XH(par, tb) + [f"st{par}_f{tb}"], writes=XN(xs))

            def TH(tb):
                n2T_gen["g"] = g
                xs = hslots[tb]
                transpose_block(xn[xs], XN(xs), n2T, "n2T", tb, 16)
                xn_busy.discard(xs)

            preY(0)
            preY(1)
            yield 3000
            yield 3000
            TY(0)
            preY(2)
            yield 2048
            TY(1)
            yield 2048
            OP(0)
            preY(3)
            yield 8192
            TY(2)
            yield 2048
            OP(1)
            preH(0)
            yield 8192
            TY(3)
            yield 2048
            OP(2)
            preH(1)
            yield 8192
            TH(0)
            yield 2048
            OP(3)
            preH(2)
            yield 8192
            TH(1)
            preH(3)
            yield 2048
            yield 3000
            TH(2)
            yield 2048
            TH(3)
            yield 2048

        def P2(g):
            assert totals is None or n2T_gen["g"] == g, ("P2 started before its n2T was complete", g, n2T_gen)
            par = g % 2
            xg = xh[par]
            sg = stat[par]
            t0 = g * 512
            sbase = g * NST
            fcnt = 0
            head = {"n": 0}

            def head_hook():
                head["n"] += 1
                keep = P.stage
                P.stage = f"FN{g - 1}"
                if head["n"] == 1 and g > 0:
                    FN_b(g - 1)
                if 2 <= head["n"] <= 5 and g > 0:
                    FN_c(g - 1, [head["n"] - 2])
                if head["n"] == 7:
                    if g > 0:
                        FN_d(g - 1)
                    if g + 1 < n_groups:
                        x_loads(g + 1)
                    xready[g + 1] = True
                P.stage = keep

            for ffh in range(2):
                P.stage = f"F1_{g}"
                for j in range(8):
                    si = sbase + ffh * 16 + j
                    stream_ensure(si + 4)
                    slot = si % 4
                    w1v = wst[slot][:].rearrange("p (k c) -> p k c", c=256)
                    for cc in range(2):
                        c = j * 2 + cc
                        if ffh == 0:
                            head_hook()
                        b = PB[fcnt % 4]
                        rs = fcnt % 2
                        fcnt += 1
                        assert totals is None or n2T_gen["g"] == g, ("n2T overwritten before FFN1 read it", g, n2T_gen)
                        for k in range(8):
                            op("pe", MM(bank[b][:], w1v[:, k, cc * 128:(cc + 1) * 128], n2T[:, k, :], k == 0, k == 7),
                               reads=NTR("n2T", [k], ALLTB) + [f"wst{slot}"], writes=[f"bank{b}"])
                            if k < 7:
                                yield 512
                        op("dve", TS(rtmp[rs][:], bank[b][:], 0.0, None, ALU.max), reads=[f"bank{b}"], writes=[f"rtmp{rs}"])
                        op("pool", TT(uT[:, c, :], rtmp[rs][:], rtmp[rs][:], ALU.mult),
                           reads=[f"rtmp{rs}"], writes=[f"uT{c}"])
                        yield 512
                P.stage = f"F2_{g}"
                for dmh in range(2):
                    for q in range(4):
                        si = sbase + ffh * 16 + 8 + dmh * 4 + q
                        stream_ensure(si + 4)
                        slot = si % 4
                        w2v = wst[slot][:].rearrange("p (r c) -> p r c", c=512)
                        for tb in range(4):
                            b = PB[tb]
                            for r in range(4):
                                c = q * 4 + r
                                op("pe", MM(bank[b][:], uT[:, c, tb * 128:(tb + 1) * 128], w2v[:, r, :],
                                            q == 0 and r == 0, q == 3 and r == 3),
                                   reads=[f"uT{c}", f"wst{slot}"], writes=[f"bank{b}"])
                                if r < 3:
                                    yield 512
                            if q == 3:
                                hs = xg[:, tb, dmh * 512:(dmh + 1) * 512]
                                op("dve", TT(hs, bank[b][:], hs, ALU.add),
                                   reads=[f"bank{b}", f"xh{par}_{tb}_{dmh}"], writes=[f"xh{par}_{tb}_{dmh}"])
                            yield 512
            P.stage = f"FN{g}"
            FN_a(g)
            yield 0

        def drain(gen):
            for _ in gen:
                pass

        xready = {}
        W = {"P1": 228000.0, "P2": 262144.0}
        meas = {}

        def drain(gen, key=None):
            for c in gen:
                if key:
                    meas[key] += c

        LAG = 1.17

        meas["P1"] = []
        meas["P2"] = []
        pos = {"P1": 0.0, "P2": 0.0}

        def chain(fn, key):
            for g in range(n_groups):
                tot = 0.0
                wg = (totals[key][g] if totals else W[key])
                for c in fn(g):
                    tot += c
                    pos[key] = g + min(tot / wg, 1.0)
                    yield c
                pos[key] = g + 1.0
                meas[key].append(tot)

        def run_pipeline(c1, c2):
            s1 = s2 = ""
            d1 = d2 = False
            while not (d1 and d2):
                take2 = (not d2) and (d1 or (pos["P2"] + LAG <= pos["P1"]))
                if take2:
                    P.stage = s2
                    try:
                        next(c2)
                    except StopIteration:
                        d2 = True
                    s2 = P.stage
                else:
                    P.stage = s1
                    try:
                        next(c1)
                    except StopIteration:
                        d1 = True
                    s1 = P.stage

        P.stage = "setup"
        setup_early()
        c1 = chain(P1, "P1")
        c2 = chain(P2, "P2")
        for _ in range(7):
            next(c1)
        P.stage = "setup"
        setup_mid()
        for _ in range(5):
            next(c1)
        P.stage = "setup"
        setup_late()
        setup_tb()
        stream_ensure(4)
        run_pipeline(c1, c2)
        P.stage = "FNlast"
        FN_b(n_groups - 1)
        FN_c(n_groups - 1)
        FN_d(n_groups - 1)
        fin = op("sp", None)
        fin.deps = list(out_dmas)
        P.emit(st)
        nc._prog_stats = P.stats
        nc._totals = dict(meas)
        nc._prog_tags = {e: [o.tag for o in P.ops[e] if o.fn is not None] for e in ENGS}
    return nc


def _consts():
    s = np.arange(128)[:, None]
    u = np.arange(256)[None, :]
    sc, uc = s // 64, u // 64
    validA = (sc <= uc) & (uc <= sc + 2)
    validB = (sc <= uc)
    slopes = 2.0 ** (-(np.arange(8) + 1.0))
    ta = np.where(validA[None], -slopes[:, None, None] * np.abs(u - s)[None].astype(np.float64), NEG)
    cst = np.zeros((128, 704), np.float32)
    cst[:, 0:128] = np.eye(128, dtype=np.float32)
    cst[:, 128:384] = validB.astype(np.float32)
    cst[:, 384:640] = np.where(validB, 0.0, NEG)
    cst[0:64, 640:704] = NEG
    idx = np.clip(u - s, -128, 128) + 128
    return ta.astype(np.float32), cst, idx


def kernel(x, norm1_g, w_in, sinks_a, rel_bias_b, out_norm_a_g, out_norm_b_g,
           w_out, norm2_g, w_ff1, w_ff2, final_norm_g):
    x = np.asarray(x, np.float32)
    B = x.shape[0]
    ta, cst, idx = _consts()
    w_in0 = np.asarray(w_in, np.float32)[0]
    perm = []
    for j in range(4):
        perm += list(range(j * 64, (j + 1) * 64)) + list(range((4 + j) * 64, (5 + j) * 64))
    cols = np.concatenate([np.array(perm), np.arange(512, PROJ)])
    w_in_p = np.ascontiguousarray(w_in0[:, cols])
    gcat = np.concatenate([np.asarray(norm1_g, np.float32)[0],
                           np.asarray(out_norm_a_g, np.float32)[0], np.asarray(out_norm_b_g, np.float32)[0],
                           np.asarray(norm2_g, np.float32)[0]])
    gT = np.ascontiguousarray(gcat.reshape(24, 128).T)
    tbraw = np.ascontiguousarray(np.asarray(rel_bias_b, np.float32)[0][:, idx])
    shared = {
        "w_in": w_in_p,
        "w_out": np.ascontiguousarray(np.asarray(w_out, np.float32)[0]),
        "w_ff1": np.ascontiguousarray(np.asarray(w_ff1, np.float32)[0]),
        "w_ff2": np.ascontiguousarray(np.asarray(w_ff2, np.float32)[0]),
        "gT": gT,
        "gf": np.asarray(final_norm_g, np.float32).reshape(1, D),
        "sinks": np.asarray(sinks_a, np.float32).reshape(1, 8),
        "tbraw": tbraw,
        "ta": ta,
        "cst": cst,
    }
    nc = build_nc(totals=build_nc()._totals)
    in_maps = [dict(shared, x=np.ascontiguousarray(x[b])) for b in range(B)]
    res = run_bass_kernel_spmd(nc, in_maps, core_ids=list(range(B)))
    return np.stack([np.asarray(r["out"], np.float32) for r in res.results], axis=0)
```

```python
from contextlib import ExitStack

import numpy as np
import concourse.bass as bass
import concourse.mybir as mybir
from concourse.bass_utils import run_bass_kernel_spmd

F32 = mybir.dt.float32
BF16 = mybir.dt.bfloat16
AF = mybir.ActivationFunctionType
ALU = mybir.AluOpType


def MM(out, lhsT, rhs, start, stop):
    return lambda e: e.matmul(out=out, lhsT=lhsT, rhs=rhs, start=start, stop=stop)


def TR(out, in_, identity):
    return lambda e: e.transpose(out=out, in_=in_, identity=identity)


def ACTF(out, in_, func, **kw):
    return lambda e: e.activation(out=out, in_=in_, func=func, **kw)


def TS(out, in0, scalar1, scalar2, op0, op1=None):
    if op1 is None:
        return lambda e: e.tensor_scalar(out=out, in0=in0, scalar1=scalar1, scalar2=scalar2, op0=op0)
    return lambda e: e.tensor_scalar(out=out, in0=in0, scalar1=scalar1, scalar2=scalar2, op0=op0, op1=op1)


def TT(out, in0, in1, op):
    return lambda e: e.tensor_tensor(out=out, in0=in0, in1=in1, op=op)


def STT(out, in0, scalar, in1, op0, op1):
    return lambda e: e.scalar_tensor_tensor(out=out, in0=in0, scalar=scalar, in1=in1, op0=op0, op1=op1)


def CP(out, in_):
    return lambda e: e.tensor_copy(out=out, in_=in_)


def RCP(out, in_):
    return lambda e: e.reciprocal(out=out, in_=in_)


def DMA(out, in_):
    return lambda e: e.dma_start(out=out, in_=in_)


def MS(ap, val):
    return lambda e: e.memset(ap, val)


S = 4096
D = 1024
NG = 8
DFF = 4096
PROJ = 2304
EPS = 1e-6
NEG = -30000.0
C_QA, C_KA, C_VA, C_QB, C_KB, C_VB = 0, 512, 640, 768, 1280, 1792

ENGS = ("pe", "act", "dve", "pool", "sp")
EPOCH = 16000


class Res:
    __slots__ = ("name", "last_w", "readers")

    def __init__(self, name):
        self.name = name
        self.last_w = None
        self.readers = []


class Op:
    __slots__ = ("eng", "fn", "deps", "sig", "dma_key", "dma_val", "has_dep", "tag")

    def __init__(self, eng, fn):
        self.eng = eng
        self.fn = fn
        self.deps = []
        self.sig = None
        self.dma_key = None
        self.dma_val = None
        self.has_dep = False


class Prog:
    def __init__(self, nc):
        self.nc = nc
        self.ops = {e: [] for e in ENGS}
        self.dma_cnt = {}
        self.res = {}
        self.stage = ""

    def R(self, name):
        r = self.res.get(name)
        if r is None:
            r = Res(name)
            self.res[name] = r
        return r

    def _rs(self, lst):
        out = []
        for x in lst:
            if x is None:
                continue
            if isinstance(x, str):
                out.append(self.R(x))
            else:
                out.extend(self._rs(x))
        return out

    def op(self, eng, fn, reads=(), writes=(), dma=None):
        o = Op(eng, fn)
        o.tag = self.stage
        reads = self._rs(reads)
        writes = self._rs(writes)
        deps = {}

        def add(p, kind):
            if p is None or p is o:
                return
            if p.dma_key is None and p.eng == eng and eng == "pe":
                return
            deps[id(p)] = p

        for r in reads:
            add(r.last_w, "raw")
        for w in writes:
            add(w.last_w, "waw")
            for rd in w.readers:
                add(rd, "war")
        for r in reads:
            if dma is None:
                r.readers = [q for q in r.readers if q.dma_key is not None or q.eng != eng]
            r.readers.append(o)
        for w in writes:
            w.last_w = o
            w.readers = []
        o.deps = list(deps.values())
        for p in o.deps:
            p.has_dep = True
        if dma is not None:
            c = self.dma_cnt.get(dma, 0) + 16
            self.dma_cnt[dma] = c
            o.dma_key = dma
            o.dma_val = c
        self.ops[eng].append(o)
        return o

    def emit(self, stack):
        nc = self.nc
        nsig = {}
        for e in ENGS:
            k = 0
            for o in self.ops[e]:
                if o.dma_key is None and o.has_dep:
                    k += 1
                    o.sig = k
            nsig[e] = k
        esem = {}
        for e in ENGS:
            ne = (nsig[e] + EPOCH - 1) // EPOCH
            esem[e] = [stack.enter_context(nc.semaphore(f"s_{e}{i}")) for i in range(ne)]
        dsem = {k: stack.enter_context(nc.semaphore(f"d_{k}")) for k in self.dma_cnt}
        self.stats = {e: [len(self.ops[e]), nsig[e], 0] for e in ENGS}
        block = stack.enter_context(nc.Block())

        def run(e, engine):
            waited = {}
            for o in self.ops[e]:
                need = {}
                for p in o.deps:
                    if p.dma_key is not None:
                        key = ("d", p.dma_key)
                        val = p.dma_val
                    else:
                        key = ("e", p.eng)
                        val = p.sig
                    if need.get(key, 0) < val:
                        need[key] = val
                for key, val in need.items():
                    if waited.get(key, 0) >= val:
                        continue
                    waited[key] = val
                    if key[0] == "d":
                        sem = dsem[key[1]]
                        wv = val
                    else:
                        ep = (val - 1) // EPOCH
                        sem = esem[key[1]][ep]
                        wv = val - ep * EPOCH
                    engine.wait_ge(sem, wv)
                    self.stats[e][2] += 1
                if o.fn is None:
                    continue
                ins = o.fn(engine)
                if o.dma_key is not None:
                    ins.then_inc(dsem[o.dma_key], 16)
                elif o.sig is not None:
                    ep = (o.sig - 1) // EPOCH
                    ins.then_inc(esem[e][ep], 1)

        @block.tensor
        def _(eng):
            run("pe", eng)

        @block.scalar
        def _(eng):
            run("act", eng)

        @block.vector
        def _(eng):
            run("dve", eng)

        @block.gpsimd
        def _(eng):
            run("pool", eng)

        @block.sync
        def _(eng):
            run("sp", eng)


def build_nc(n_groups=NG, debug=False, totals=None):
    nc = bass.Bass("TRN2", target_bir_lowering=False)
    dram = lambda n, s, d=F32, kind="ExternalInput": nc.dram_tensor(n, s, d, kind=kind).ap()
    x_d = dram("x", [S, D])
    win_d = dram("w_in", [D, PROJ])
    wout_d = dram("w_out", [D, D])
    w1_d = dram("w_ff1", [D, DFF])
    w2_d = dram("w_ff2", [DFF, D])
    gT_d = dram("gT", [128, 24])
    gf_d = dram("gf", [1, D])
    sinks_d = dram("sinks", [1, 8])
    tbraw_d = dram("tbraw", [8, 128, 256])
    ta_d = dram("ta", [8, 128, 256])
    cst_d = dram("cst", [128, 128 + 256 + 256 + 64])
    out_d = dram("out", [S, D], F32, kind="ExternalOutput")
    w1bf_d = dram("w1bf", [16, 128, 8, 256], BF16, kind="Internal")
    w2bf_d = dram("w2bf", [16, 128, 4, 512], BF16, kind="Internal")
    dbg = {}
    if debug:
        dbg["qp"] = dram("dbg_qp", [8, 128, 512], F32, kind="ExternalOutput")
        dbg["kt"] = dram("dbg_kt", [128, 5, 1024], F32, kind="ExternalOutput")
        dbg["v"] = dram("dbg_v", [128, 8 * 10 * 65], F32, kind="ExternalOutput")
        dbg["y"] = dram("dbg_y", [128, 4 * 1024], F32, kind="ExternalOutput")
        dbg["h"] = dram("dbg_h", [128, 4 * 1024], F32, kind="ExternalOutput")

    with ExitStack() as st:
        sb = lambda n, s, d: st.enter_context(nc.sbuf_tensor(n, s, d))
        win_sb = sb("win_sb", [128, 8, PROJ], BF16)
        wout_sb = sb("wout_sb", [128, 8, D], BF16)
        wst = [sb(f"wst{i}", [128, 2048], BF16) for i in range(4)]
        xh = [sb(f"xh{i}", [128, 4, D], F32) for i in range(2)]
        xn = [sb(f"xn{i}", [128, D], BF16) for i in range(2)]
        xtmp = [sb(f"xtmp{i}", [128, D], F32) for i in range(2)]
        nT = sb("nT", [128, 8, 512], BF16)
        n2T = sb("n2T", [128, 8, 512], BF16)
        QP = [sb(f"qp{i}", [128, 512], BF16) for i in range(8)]
        KT = sb("KT", [128, 5, 1024], BF16)
        V = sb("V", [128, 8, 10, 65], BF16)
        PT = [sb(f"pt{i}", [128, 2560], BF16) for i in range(2)]
        ybuf = sb("ybuf", [128, 4, D], BF16)
        uT = sb("uT", [128, 16, 512], BF16)
        rtmp = [sb(f"rtmp{i}", [128, 512], BF16) for i in range(2)]
        TA = sb("TA", [128, 8, 256], BF16)
        TB = sb("TB", [128, 8, 256], BF16)
        ident = sb("ident", [128, 128], BF16)
        mask9 = sb("mask9", [128, 64], BF16)
        gf = sb("gf_sb", [128, D], F32)
        gT = sb("gT_sb", [128, 24], F32)
        esink = sb("esink", [128, 8], F32)
        mhalf = sb("mhalf", [128, 8], F32)
        stat = [sb(f"stat{i}", [128, 40], F32) for i in range(2)]
        rl = [sb(f"rl{i}", [128, 4], F32) for i in range(2)]
        bank = [st.enter_context(nc.psum_tensor(f"bank{i}", [128, 512], F32)) for i in range(8)]
        PA = [0, 1, 2, 3]
        PB = [4, 5, 6, 7]

        P = Prog(nc)
        op = P.op

        WBLK = {"KAV": (512, 768), "KB": (1280, 1792), "VB": (1792, 2304), "QA": (0, 512), "QB": (768, 1280)}

        def win_load(name):
            c0, c1 = WBLK[name]
            op("pool", DMA(win_sb[:, :, c0:c1], win_d[:, c0:c1].rearrange("(k p) c -> p k c", p=128)),
               writes=[f"win_{name}"], dma=f"c_win_{name}")

        def win_res(col0):
            for name, (c0, c1) in WBLK.items():
                if c0 <= col0 < c1:
                    return f"win_{name}"
            raise KeyError(col0)

        tbtmp = uT[:].rearrange("p a b -> p (a b)").bitcast(F32)
        UT_ALL = [f"uT{c}" for c in range(16)]
        tbraw = tbtmp[:, 0:2048].rearrange("p (h u) -> p h u", u=256)
        vneg = tbtmp[:, 2048:2560]

        def setup_early():
            op("sp", DMA(gT[:], gT_d), writes=["gT"], dma="c_gT")
            op("pool", MS(mhalf[:], -0.5), writes=["mhalf"])
            op("pool", DMA(ident[:], cst_d[:, 0:128]), writes=["ident"], dma="c_id")
            win_load("KAV")
            win_load("KB")

        def setup_mid():
            x_loads(0)
            win_load("VB")
            win_load("QA")
            op("pool", MS(V[:].rearrange("p a b c -> p (a b c)"), 1.0),
               writes=[f"V{i}{sfx}" for i in range(8) for sfx in ("_a", "_b")])
            for i in range(8):
                op("pool", MS(QP[i][:], 0.0), writes=[f"qp{i}"])
            op("pool", DMA(TA[:], ta_d.rearrange("h p u -> p h u")), writes=["TA"], dma="c_ta")
            op("pool", DMA(mask9[:], cst_d[:, 640:704]), writes=["mask9"], dma="c_m9")
            win_load("QB")
            op("sp", DMA(esink[:], sinks_d.partition_broadcast(128)), writes=["esink"], dma="c_sink")
            op("sp", DMA(gf[:], gf_d.partition_broadcast(128)), writes=["gf"], dma="c_gf")
            op("sp", DMA(tbraw, tbraw_d.rearrange("h p u -> p h u")), writes=UT_ALL, dma="c_tb")
            op("sp", DMA(vneg, cst_d[:, 128:640]), writes=["vneg"], dma="c_vn")

        def setup_tb():
            for h in range(8):
                op("dve", TS(tbraw[:, h, :], tbraw[:, h, :], tbraw[:, h, 255:256], None, ALU.subtract),
                   reads=UT_ALL, writes=UT_ALL)
                op("dve", TT(tbraw[:, h, :], tbraw[:, h, :], vneg[:, 0:256], ALU.mult),
                   reads=UT_ALL + ["vneg"], writes=UT_ALL)
                op("dve", TT(TB[:, h, :], tbraw[:, h, :], vneg[:, 256:512], ALU.add),
                   reads=UT_ALL + ["vneg"], writes=["TB"])
            op("act", ACTF(esink[:], esink[:], AF.Exp), reads=["esink"], writes=["esink"])

        def setup_late():
            for k in range(8):
                op("pool", DMA(wout_sb[:, k, :], wout_d[k * 128:(k + 1) * 128, :]), writes=[f"wout{k}"], dma=f"c_wout{k}")

        stream = []
        for ffh in range(2):
            for j in range(8):
                stream.append(("w1", ffh * 8 + j))
            for dmh in range(2):
                for q in range(4):
                    stream.append(("w2", (ffh * 2 + dmh) * 4 + q))
        NST = len(stream)
        st_state = {"next": 0}

        def stream_ensure(upto):
            upto = min(upto, NST * n_groups)
            while st_state["next"] < upto:
                s_ = st_state["next"]
                kind, idx = stream[s_ % NST]
                slot = s_ % 4
                if kind == "w1":
                    scr = w1bf_d[idx].rearrange("p k c -> p (k c)")
                    rd = f"w1bf{idx}"
                    src32 = w1_d[:, idx * 256:(idx + 1) * 256].rearrange("(k p) c -> p k c", p=128)
                    dst3 = wst[slot][:].rearrange("p (k c) -> p k c", c=256)
                else:
                    scr = w2bf_d[idx].rearrange("p r c -> p (r c)")
                    rd = f"w2bf{idx}"
                    ffh, dmh, q = idx // 8, (idx // 4) % 2, idx % 4
                    r0 = (ffh * 16 + q * 4) * 128
                    src32 = w2_d[r0:r0 + 512, dmh * 512:(dmh + 1) * 512].rearrange("(r p) c -> p r c", p=128)
                    dst3 = wst[slot][:].rearrange("p (r c) -> p r c", c=512)
                if s_ < NST:
                    op("pool", DMA(dst3, src32), writes=[f"wst{slot}"], dma=f"wstc{slot}")
                    if n_groups > 1:
                        op("sp", DMA(scr, wst[slot][:]), reads=[f"wst{slot}"], writes=[rd], dma=f"wb{slot}")
                else:
                    op("sp", DMA(wst[slot][:], scr), reads=[rd], writes=[f"wst{slot}"], dma=f"wst{slot}")
                st_state["next"] += 1

        rot = {"A": 0, "xn": 0}
        NXN = len(xn)

        def next_bankA():
            b = PA[rot["A"] % 4]
            rot["A"] += 1
            return b

        xn_busy = set()

        def xn_slot():
            for _ in range(NXN):
                s_ = rot["xn"] % NXN
                rot["xn"] += 1
                if s_ not in xn_busy:
                    xn_busy.add(s_)
                    return s_
            raise AssertionError("no free xn slot")

        def XN(s_):
            return [f"xn{s_}_0", f"xn{s_}_1"]

        def XH(par, tb):
            return [f"xh{par}_{tb}_0", f"xh{par}_{tb}_1"]

        def NTR(name, ks, tbs):
            return [f"{name}{k}_{t}" for k in ks for t in tbs]

        def rstd_from_ssq(sq_ap, r_ap, n, res_sq, res_r):
            k = sq_ap.shape[1]
            op("pool", TS(sq_ap, sq_ap, 1.0 / n, EPS, ALU.mult, ALU.add), reads=[res_sq], writes=[res_sq])
            op("pool", TT(r_ap, sq_ap, mhalf[:, 0:k], ALU.pow), reads=[res_sq, "mhalf"], writes=[res_r])

        def transpose_block(src_bf, src_res, dst, dname, tb, gcol):
            b = PA[rot["A"] % 2]
            rot["A"] += 1
            pb = bank[b][:].bitcast(BF16)
            for k in range(8):
                op("pe", TR(pb[:, k * 128:(k + 1) * 128], src_bf[:, k * 128:(k + 1) * 128], ident[:]),
                   reads=[src_res, "ident"], writes=[f"bank{b}"])
            for k in range(8):
                op("dve", TS(dst[:, k, tb * 128:(tb + 1) * 128], pb[:, k * 128:(k + 1) * 128],
                             gT[:, gcol + k:gcol + k + 1], None, ALU.mult),
                   reads=[f"bank{b}", "gT"], writes=[f"{dname}{k}_{tb}"])

        out_dmas = []
        ALLTB = range(4)

        xloaded = set()
        n2T_gen = {"g": -1}

        def x_loads(g):
            par = g % 2
            for tb in range(4):
                xloaded.add((g, tb))
                op("sp", DMA(xh[par][:, tb, :], x_d[g * 512 + tb * 128:g * 512 + (tb + 1) * 128, :]),
                   writes=XH(par, tb), dma=f"x{par}{tb}")

        def FN_a(g):
            par = g % 2
            sg = stat[par]
            junk = uT[:, 0:2, :].rearrange("p a b -> p (a b)")
            for tb in range(4):
                op("act", ACTF(junk, xh[par][:, tb, :], AF.Square, accum_out=sg[:, 32 + tb:33 + tb]),
                   reads=XH(par, tb), writes=["uT0", "uT1", f"st{par}_g{tb}"])

        def FN_b(g):
            par = g % 2
            sg = stat[par]
            for tb in range(4):
                rstd_from_ssq(sg[:, 32 + tb:33 + tb], sg[:, 36 + tb:37 + tb], D, f"st{par}_g{tb}", f"st{par}_h{tb}")

        def FN_c(g, tbs=None):
            par = g % 2
            sg = stat[par]
            xg = xh[par]
            t0 = g * 512
            for tb in (range(4) if tbs is None else tbs):
                op("dve", STT(xg[:, tb, :], xg[:, tb, :], sg[:, 36 + tb:37 + tb], gf[:], ALU.mult, ALU.mult),
                   reads=XH(par, tb) + [f"st{par}_h{tb}", "gf"], writes=XH(par, tb))

        def FN_d(g):
            par = g % 2
            xg = xh[par]
            t0 = g * 512
            for tb in range(4):
                o = op("sp", DMA(out_d[t0 + tb * 128:t0 + (tb + 1) * 128, :], xg[:, tb, :]),
                       reads=XH(par, tb), dma=f"o{par}{tb}")
                out_dmas.append(o)

        def P1(g):
            par = g % 2
            xg = xh[par]
            sg = stat[par]
            t0 = g * 512
            P.stage = f"A{g}"
            slots = {}

            def xt_load(tb):
                xt = tb % 2
                q_ = "sp" if tb < 2 else "pool"
                op(q_, DMA(xtmp[xt][:], x_d[t0 + tb * 128:t0 + (tb + 1) * 128, :]),
                   writes=[f"xtmp{xt}"], dma=f"xt{xt}_{q_}")

            def preA(tb):
                xs = xn_slot()
                slots[tb] = xs
                xt = tb % 2
                if tb < 2:
                    xt_load(tb)
                op("act", ACTF(xn[xs][:], xtmp[xt][:], AF.Square, accum_out=sg[:, tb:tb + 1]),
                   reads=[f"xtmp{xt}"], writes=XN(xs) + [f"st{par}_a{tb}"])
                rstd_from_ssq(sg[:, tb:tb + 1], sg[:, 4 + tb:5 + tb], D, f"st{par}_a{tb}", f"st{par}_b{tb}")
                if tb % 2 == 0:
                    op("act", ACTF(xn[xs][:], xtmp[xt][:], AF.Copy, scale=sg[:, 4 + tb:5 + tb]),
                       reads=[f"xtmp{xt}", f"st{par}_b{tb}"], writes=XN(xs))
                else:
                    op("pool", TS(xn[xs][:], xtmp[xt][:], sg[:, 4 + tb:5 + tb], 1.0, ALU.mult, ALU.mult),
                       reads=[f"xtmp{xt}", f"st{par}_b{tb}"], writes=XN(xs))
                if tb + 2 < 4:
                    xt_load(tb + 2)

            def TA_(tb):
                xs = slots[tb]
                transpose_block(xn[xs], XN(xs), nT, "nT", tb, 0)
                xn_busy.discard(xs)

            preA(0)
            preA(1)
            for _ in range(3):
                yield 6000
            TA_(0)
            preA(2)
            yield 5000
            TA_(1)
            preA(3)
            yield 5000
            TA_(2)
            yield 4000
            TA_(3)
            yield 2048

            P.stage = f"B{g}"

            def proj_fm(col0, evac):
                b = next_bankA()
                for k in range(8):
                    op("pe", MM(bank[b][:], win_sb[:, k, col0:col0 + 128], nT[:, k, :], k == 0, k == 7),
                       reads=NTR("nT", [k], ALLTB) + [win_res(col0)], writes=[f"bank{b}"])
                evac(b)

            def evac_k(chunk):
                def f(b):
                    op("act", ACTF(KT[:, chunk, par * 512:(par + 1) * 512], bank[b][:], AF.Copy),
                       reads=[f"bank{b}"], writes=[f"KT{par}_{chunk}"])
                return f

            def evac_q(t_lo, t_hi):
                def f(b):
                    op("act", ACTF(QP[t_lo][0:64, :], bank[b][0:64, :], AF.Copy, scale=0.125),
                       reads=[f"bank{b}"], writes=[f"qp{t_lo}"])
                    op("act", ACTF(QP[t_hi][64:128, :], bank[b][64:128, :], AF.Copy, scale=0.125),
                       reads=[f"bank{b}"], writes=[f"qp{t_hi}"])
                return f

            proj_fm(C_KA, evac_k(0))
            yield 4096
            for m in range(4):
                proj_fm(C_KB + m * 128, evac_k(1 + m))
                yield 4096
            for tb in range(4):
                slot = (g * 4 + tb) % 8
                b1 = next_bankA()
                b2 = next_bankA()
                for k in range(8):
                    op("pe", MM(bank[b1][:, 0:128], nT[:, k, tb * 128:(tb + 1) * 128],
                                win_sb[:, k, C_VA:C_VA + 128], k == 0, k == 7),
                       reads=[f"nT{k}_{tb}", "win_KAV"], writes=[f"bank{b1}"])
                for k in range(8):
                    op("pe", MM(bank[b2][:], nT[:, k, tb * 128:(tb + 1) * 128],
                                win_sb[:, k, C_VB:C_VB + 512], k == 0, k == 7),
                       reads=[f"nT{k}_{tb}", "win_VB"], writes=[f"bank{b2}"])
                op("dve", CP(V[:, slot, 0:2, 0:64], bank[b1][:, 0:128].rearrange("p (h d) -> p h d", d=64)),
                   reads=[f"bank{b1}"], writes=[f"V{slot}_a"])
                op("dve", CP(V[:, slot, 2:10, 0:64], bank[b2][:].rearrange("p (h d) -> p h d", d=64)),
                   reads=[f"bank{b2}"], writes=[f"V{slot}_b"])
                yield 5120
            for j in range(4):
                proj_fm(C_QA + j * 128, evac_q(j, 4 + j))
                yield 4096

            def attention(mixer):
                W = 3 if mixer == "A" else 9
                TX = TA if mixer == "A" else TB
                txres = "TA" if mixer == "A" else "TB"
                vsuf = "_a" if mixer == "A" else "_b"
                back = 1 if mixer == "A" else 4
                kbs = [kb for kb in range(4 * g - back, 4 * g + 4) if kb >= 0]
                info = {}
                off = 0
                for ki, kb in enumerate(kbs):
                    cs = max(2 * kb, 8 * g)
                    ce = min(2 * kb + W, 8 * g + 7)
                    u0 = 64 * (cs - 2 * kb)
                    u1 = 64 * (ce + 1 - 2 * kb)
                    info[kb] = (u0, u1, off, 64 * (cs - 8 * g), ki)
                    off += u1 - u0
                sbank = [PA[0], PA[1]]
                obank = [PA[2], PA[3]]
                cnt = {"s": 0}

                def qk(h, fill=()):
                    fill = list(fill)
                    nkb = len(kbs)
                    ptp = h % 2
                    if mixer == "A":
                        qt = h
                        kchunk = 0
                    else:
                        qt = (h // 2) if h % 2 == 0 else 4 + h // 2
                        kchunk = 1 + h // 2
                    for kidx, kb in enumerate(kbs):
                        u0, u1, poff, qc0, ki = info[kb]
                        n = u1 - u0
                        b = sbank[cnt["s"] % 2]
                        cnt["s"] += 1
                        kcol = (kb % 8) * 128
                        kpar = (kb // 4) % 2
                        mms = [(bank[b][:, 0:n], KT[:, kchunk, kcol:kcol + 128], QP[qt][:, qc0:qc0 + n],
                                [f"KT{kpar}_{kchunk}", f"qp{qt}"])]
                        if u0 < 256:
                            ub1 = min(u1, 256)
                            mms.append((bank[b][:, 0:ub1 - u0], ident[:], TX[:, h, u0:ub1], ["ident", txres]))
                        if mixer == "B" and u1 == 640:
                            mms.append((bank[b][:, 576 - u0:640 - u0], ident[:], mask9[:], ["ident", "mask9"]))
                        cost = 0
                        for i, (o_, l_, r_, rd) in enumerate(mms):
                            op("pe", MM(o_, l_, r_, i == 0, i == len(mms) - 1), reads=rd, writes=[f"bank{b}"])
                            cost += o_.shape[1]
                        op("act", ACTF(PT[ptp][:, poff:poff + n], bank[b][:, 0:n], AF.Exp),
                           reads=[f"bank{b}"], writes=[f"pt{ptp}_{ki}"])
                        left = nkb - kidx
                        take = (len(fill) + left - 1) // left
                        for f in fill[:take]:
                            f()
                        fill = fill[take:]
                        yield cost + 100 * take
                    for f in fill:
                        f()

                def pv(h):
                    ptp = h % 2
                    b = obank[h % 2]
                    hv = (h // 4) if mixer == "A" else 2 + h
                    ycol = (0 if mixer == "A" else 512) + h * 64
                    ems = []
                    for qbl in range(4):
                        qb = 4 * g + qbl
                        ks = [kb for kb in range(qb - back, qb + 1) if kb >= 0]
                        for i, kb in enumerate(ks):
                            u0, u1, poff, qc0, ki = info[kb]
                            c0 = poff + 128 * (qb - kb) - u0
                            slot = kb % 8

                            def em(qbl=qbl, c0=c0, slot=slot, ki=ki, first=(i == 0), last=(i == len(ks) - 1)):
                                op("pe", MM(bank[b][:, qbl * 65:(qbl + 1) * 65], PT[ptp][:, c0:c0 + 128],
                                            V[:, slot, hv, :], first, last),
                                   reads=[f"pt{ptp}_{ki}", f"V{slot}{vsuf}"], writes=[f"bank{b}"])
                            ems.append(em)
                    ems.append(lambda: pv_epilogue(h, b, ycol))
                    return ems

                def pv_epilogue(h, b, ycol):
                    ov = bank[b][:, 0:260].rearrange("p (q d) -> p q d", d=65)
                    lsum = ov[:, :, 64:65].rearrange("p q o -> p (q o)")
                    rli = rl[h % 2]
                    rres = f"rl{h % 2}"
                    if mixer == "A":
                        op("dve", TS(rli[:], lsum, esink[:, h:h + 1], None, ALU.add),
                           reads=[f"bank{b}", "esink"], writes=[rres])
                        op("dve", RCP(rli[:], rli[:]), reads=[rres], writes=[rres])
                    else:
                        op("dve", RCP(rli[:], lsum), reads=[f"bank{b}"], writes=[rres])
                    op("dve", TT(ybuf[:, :, ycol:ycol + 64], ov[:, :, 0:64],
                                 rli[:].unsqueeze(2).to_broadcast([128, 4, 64]), ALU.mult),
                       reads=[f"bank{b}", rres], writes=[f"y{mixer}{h}"])

                yield from qk(0)
                for h in range(1, 8):
                    yield from qk(h, pv(h - 1))
                ems = pv(7)
                for f in ems:
                    f()
                yield 100 * len(ems)

            P.stage = f"CA{g}"
            yield from attention("A")
            P.stage = f"QB{g}"
            for m in range(4):
                proj_fm(C_QB + m * 128, evac_q(m, 4 + m))
                yield 4096
            P.stage = f"CB{g}"
            yield from attention("B")

            P.stage = f"D{g}"
            yslots = {}
            hslots = {}

            def preY(tb):
                xs = xn_slot()
                yslots[tb] = xs
                for m in range(2):
                    mx = "AB"[m]
                    op("act", ACTF(xn[xs][:, m * 512:(m + 1) * 512], ybuf[:, tb, m * 512:(m + 1) * 512],
                                   AF.Square, accum_out=sg[:, 8 + 2 * tb + m:9 + 2 * tb + m]),
                       reads=[f"y{mx}{h}" for h in range(8)], writes=[f"xn{xs}_{m}", f"st{par}_c{tb}"])
                rstd_from_ssq(sg[:, 8 + 2 * tb:10 + 2 * tb], sg[:, 16 + 2 * tb:18 + 2 * tb], 512,
                              f"st{par}_c{tb}", f"st{par}_d{tb}")
                for m in range(2):
                    mx = "AB"[m]
                    op("pool", TS(xn[xs][:, m * 512:(m + 1) * 512], ybuf[:, tb, m * 512:(m + 1) * 512],
                                  sg[:, 16 + 2 * tb + m:17 + 2 * tb + m], 1.0, ALU.mult, ALU.mult),
                       reads=[f"y{mx}{h}" for h in range(8)] + [f"st{par}_d{tb}"], writes=[f"xn{xs}_{m}"])

            def TY(tb):
                xs = yslots[tb]
                transpose_block(xn[xs], XN(xs), nT, "nT", tb, 8)
                xn_busy.discard(xs)

            def OP(tb):
                assert (g, tb) in xloaded, ("x for the residual not loaded yet", g, tb)
                for dmh in range(2):
                    b = next_bankA()
                    for k in range(8):
                        op("pe", MM(bank[b][:], nT[:, k, tb * 128:(tb + 1) * 128],
                                    wout_sb[:, k, dmh * 512:(dmh + 1) * 512], k == 0, k == 7),
                           reads=[f"nT{k}_{tb}", f"wout{k}"], writes=[f"bank{b}"])
                    hs = xg[:, tb, dmh * 512:(dmh + 1) * 512]
                    op("dve", TT(hs, bank[b][:], hs, ALU.add),
                       reads=[f"bank{b}", f"xh{par}_{tb}_{dmh}"], writes=[f"xh{par}_{tb}_{dmh}"])

            def preH(tb):
                xs = xn_slot()
                hslots[tb] = xs
                op("act", ACTF(xn[xs][:], xg[:, tb, :], AF.Square, accum_out=sg[:, 24 + tb:25 + tb]),
                   reads=XH(par, tb), writes=XN(xs) + [f"st{par}_e{tb}"])
                rstd_from_ssq(sg[:, 24 + tb:25 + tb], sg[:, 28 + tb:29 + tb], D, f"st{par}_e{tb}", f"st{par}_f{tb}")
                if tb % 2 == 0:
                    op("act", ACTF(xn[xs][:], xg[:, tb, :], AF.Copy, scale=sg[:, 28 + tb:29 + tb]),
                       reads=XH(par, tb) + [f"st{par}_f{tb}"], writes=XN(xs))
                else:
                    op("pool", TS(xn[xs][:], xg[:, tb, :], sg[:, 28 + tb:29 + tb], 1.0, ALU.mult, ALU.mult),
                       reads=XH(par, tb) + [f"st{par}_f{tb}"], writes=XN(xs))

            def TH(tb):
                n2T_gen["g"] = g
                xs = hslots[tb]
                transpose_block(xn[xs], XN(xs), n2T, "n2T", tb, 16)
                xn_busy.discard(xs)

            preY(0)
            preY(1)
            yield 3000
            yield 3000
            TY(0)
            preY(2)
            yield 2048
            TY(1)
            yield 2048
            OP(0)
            preY(3)
            yield 8192
            TY(2)
            yield 2048
            OP(1)
            preH(0)
            yield 8192
            TY(3)
            yield 2048
            OP(2)
            preH(1)
            yield 8192
            TH(0)
            yield 2048
            OP(3)
            preH(2)
            yield 8192
            TH(1)
            preH(3)
            yield 2048
            yield 3000
            TH(2)
            yield 2048
            TH(3)
            yield 2048

        def P2(g):
            assert totals is None or n2T_gen["g"] == g, ("P2 started before its n2T was complete", g, n2T_gen)
            par = g % 2
            xg = xh[par]
            sg = stat[par]
            t0 = g * 512
            sbase = g * NST
            fcnt = 0
            head = {"n": 0}

            def head_hook():
                head["n"] += 1
                keep = P.stage
                P.stage = f"FN{g - 1}"
                if head["n"] == 1 and g > 0:
                    FN_b(g - 1)
                if 2 <= head["n"] <= 5 and g > 0:
                    FN_c(g - 1, [head["n"] - 2])
                if head["n"] == 7:
                    if g > 0:
                        FN_d(g - 1)
                    if g + 1 < n_groups:
                        x_loads(g + 1)
                    xready[g + 1] = True
                P.stage = keep

            for ffh in range(2):
                P.stage = f"F1_{g}"
                for j in range(8):
                    si = sbase + ffh * 16 + j
                    stream_ensure(si + 4)
                    slot = si % 4
                    w1v = wst[slot][:].rearrange("p (k c) -> p k c", c=256)
                    for cc in range(2):
                        c = j * 2 + cc
                        if ffh == 0:
                            head_hook()
                        b = PB[fcnt % 4]
                        rs = fcnt % 2
                        fcnt += 1
                        assert totals is None or n2T_gen["g"] == g, ("n2T overwritten before FFN1 read it", g, n2T_gen)
                        for k in range(8):
                            op("pe", MM(bank[b][:], w1v[:, k, cc * 128:(cc + 1) * 128], n2T[:, k, :], k == 0, k == 7),
                               reads=NTR("n2T", [k], ALLTB) + [f"wst{slot}"], writes=[f"bank{b}"])
                            if k < 7:
                                yield 512
                        op("dve", TS(rtmp[rs][:], bank[b][:], 0.0, None, ALU.max), reads=[f"bank{b}"], writes=[f"rtmp{rs}"])
                        op("pool", TT(uT[:, c, :], rtmp[rs][:], rtmp[rs][:], ALU.mult),
                           reads=[f"rtmp{rs}"], writes=[f"uT{c}"])
                        yield 512
                P.stage = f"F2_{g}"
                for dmh in range(2):
                    for q in range(4):
                        si = sbase + ffh * 16 + 8 + dmh * 4 + q
                        stream_ensure(si + 4)
                        slot = si % 4
                        w2v = wst[slot][:].rearrange("p (r c) -> p r c", c=512)
                        for tb in range(4):
                            b = PB[tb]
                            for r in range(4):
                                c = q * 4 + r
                                op("pe", MM(bank[b][:], uT[:, c, tb * 128:(tb + 1) * 128], w2v[:, r, :],
                                            q == 0 and r == 0, q == 3 and r == 3),
                                   reads=[f"uT{c}", f"wst{slot}"], writes=[f"bank{b}"])
                                if r < 3:
                                    yield 512
                            if q == 3:
                                hs = xg[:, tb, dmh * 512:(dmh + 1) * 512]
                                op("dve", TT(hs, bank[b][:], hs, ALU.add),
                                   reads=[f"bank{b}", f"xh{par}_{tb}_{dmh}"], writes=[f"xh{par}_{tb}_{dmh}"])
                            yield 512
            P.stage = f"FN{g}"
            FN_a(g)
            yield 0

        def drain(gen):
            for _ in gen:
                pass

        xready = {}
        W = {"P1": 228000.0, "P2": 262144.0}
        meas = {}

        def drain(gen, key=None):
            for c in gen:
                if key:
                    meas[key] += c

        LAG = 1.16

        meas["P1"] = []
        meas["P2"] = []
        pos = {"P1": 0.0, "P2": 0.0}

        def chain(fn, key):
            for g in range(n_groups):
                tot = 0.0
                wg = (totals[key][g] if totals else W[key])
                for c in fn(g):
                    tot += c
                    pos[key] = g + min(tot / wg, 1.0)
                    yield c
                pos[key] = g + 1.0
                meas[key].append(tot)

        def run_pipeline(c1, c2):
            s1 = s2 = ""
            d1 = d2 = False
            while not (d1 and d2):
                take2 = (not d2) and (d1 or (pos["P2"] + LAG <= pos["P1"]))
                if take2:
                    P.stage = s2
                    try:
                        next(c2)
                    except StopIteration:
                        d2 = True
                    s2 = P.stage
                else:
                    P.stage = s1
                    try:
                        next(c1)
                    except StopIteration:
                        d1 = True
                    s1 = P.stage

        P.stage = "setup"
        setup_early()
        c1 = chain(P1, "P1")
        c2 = chain(P2, "P2")
        for _ in range(7):
            next(c1)
        P.stage = "setup"
        setup_mid()
        for _ in range(5):
            next(c1)
        P.stage = "setup"
        setup_late()
        setup_tb()
        stream_ensure(4)
        run_pipeline(c1, c2)
        P.stage = "FNlast"
        FN_b(n_groups - 1)
        FN_c(n_groups - 1)
        FN_d(n_groups - 1)
        fin = op("sp", None)
        fin.deps = list(out_dmas)
        P.emit(st)
        nc._prog_stats = P.stats
        nc._totals = dict(meas)
        nc._prog_tags = {e: [o.tag for o in P.ops[e] if o.fn is not None] for e in ENGS}
    return nc


def _consts():
    s = np.arange(128)[:, None]
    u = np.arange(256)[None, :]
    sc, uc = s // 64, u // 64
    validA = (sc <= uc) & (uc <= sc + 2)
    validB = (sc <= uc)
    slopes = 2.0 ** (-(np.arange(8) + 1.0))
    ta = np.where(validA[None], -slopes[:, None, None] * np.abs(u - s)[None].astype(np.float64), NEG)
    cst = np.zeros((128, 704), np.float32)
    cst[:, 0:128] = np.eye(128, dtype=np.float32)
    cst[:, 128:384] = validB.astype(np.float32)
    cst[:, 384:640] = np.where(validB, 0.0, NEG)
    cst[0:64, 640:704] = NEG
    idx = np.clip(u - s, -128, 128) + 128
    return ta.astype(np.float32), cst, idx


def kernel(x, norm1_g, w_in, sinks_a, rel_bias_b, out_norm_a_g, out_norm_b_g,
           w_out, norm2_g, w_ff1, w_ff2, final_norm_g):
    x = np.asarray(x, np.float32)
    B = x.shape[0]
    ta, cst, idx = _consts()
    w_in0 = np.asarray(w_in, np.float32)[0]
    perm = []
    for j in range(4):
        perm += list(range(j * 64, (j + 1) * 64)) + list(range((4 + j) * 64, (5 + j) * 64))
    cols = np.concatenate([np.array(perm), np.arange(512, PROJ)])
    w_in_p = np.ascontiguousarray(w_in0[:, cols])
    gcat = np.concatenate([np.asarray(norm1_g, np.float32)[0],
                           np.asarray(out_norm_a_g, np.float32)[0], np.asarray(out_norm_b_g, np.float32)[0],
                           np.asarray(norm2_g, np.float32)[0]])
    gT = np.ascontiguousarray(gcat.reshape(24, 128).T)
    tbraw = np.ascontiguousarray(np.asarray(rel_bias_b, np.float32)[0][:, idx])
    shared = {
        "w_in": w_in_p,
        "w_out": np.ascontiguousarray(np.asarray(w_out, np.float32)[0]),
        "w_ff1": np.ascontiguousarray(np.asarray(w_ff1, np.float32)[0]),
        "w_ff2": np.ascontiguousarray(np.asarray(w_ff2, np.float32)[0]),
        "gT": gT,
        "gf": np.asarray(final_norm_g, np.float32).reshape(1, D),
        "sinks": np.asarray(sinks_a, np.float32).reshape(1, 8),
        "tbraw": tbraw,
        "ta": ta,
        "cst": cst,
    }
    nc = build_nc(totals=build_nc()._totals)
    in_maps = [dict(shared, x=np.ascontiguousarray(x[b])) for b in range(B)]
    res = run_bass_kernel_spmd(nc, in_maps, core_ids=list(range(B)))
    return np.stack([np.asarray(r["out"], np.float32) for r in res.results], axis=0)
```

```python
from contextlib import ExitStack

import numpy as np
import concourse.bass as bass
import concourse.mybir as mybir
from concourse.bass_utils import run_bass_kernel_spmd

F32 = mybir.dt.float32
BF16 = mybir.dt.bfloat16
AF = mybir.ActivationFunctionType
ALU = mybir.AluOpType


def MM(out, lhsT, rhs, start, stop):
    return lambda e: e.matmul(out=out, lhsT=lhsT, rhs=rhs, start=start, stop=stop)


def TR(out, in_, identity):
    return lambda e: e.transpose(out=out, in_=in_, identity=identity)


def ACTF(out, in_, func, **kw):
    return lambda e: e.activation(out=out, in_=in_, func=func, **kw)


def TS(out, in0, scalar1, scalar2, op0, op1=None):
    if op1 is None:
        return lambda e: e.tensor_scalar(out=out, in0=in0, scalar1=scalar1, scalar2=scalar2, op0=op0)
    return lambda e: e.tensor_scalar(out=out, in0=in0, scalar1=scalar1, scalar2=scalar2, op0=op0, op1=op1)


def TT(out, in0, in1, op):
    return lambda e: e.tensor_tensor(out=out, in0=in0, in1=in1, op=op)


def STT(out, in0, scalar, in1, op0, op1):
    return lambda e: e.scalar_tensor_tensor(out=out, in0=in0, scalar=scalar, in1=in1, op0=op0, op1=op1)


def CP(out, in_):
    return lambda e: e.tensor_copy(out=out, in_=in_)


def RCP(out, in_):
    return lambda e: e.reciprocal(out=out, in_=in_)


def DMA(out, in_):
    return lambda e: e.dma_start(out=out, in_=in_)


def MS(ap, val):
    return lambda e: e.memset(ap, val)


S = 4096
D = 1024
NG = 8
DFF = 4096
PROJ = 2304
EPS = 1e-6
NEG = -30000.0
C_QA, C_KA, C_VA, C_QB, C_KB, C_VB = 0, 512, 640, 768, 1280, 1792

ENGS = ("pe", "act", "dve", "pool", "sp")
EPOCH = 16000


class Res:
    __slots__ = ("name", "last_w", "readers")

    def __init__(self, name):
        self.name = name
        self.last_w = None
        self.readers = []


class Op:
    __slots__ = ("eng", "fn", "deps", "sig", "dma_key", "dma_val", "has_dep", "tag")

    def __init__(self, eng, fn):
        self.eng = eng
        self.fn = fn
        self.deps = []
        self.sig = None
        self.dma_key = None
        self.dma_val = None
        self.has_dep = False


class Prog:
    def __init__(self, nc):
        self.nc = nc
        self.ops = {e: [] for e in ENGS}
        self.dma_cnt = {}
        self.res = {}
        self.stage = ""

    def R(self, name):
        r = self.res.get(name)
        if r is None:
            r = Res(name)
            self.res[name] = r
        return r

    def _rs(self, lst):
        out = []
        for x in lst:
            if x is None:
                continue
            if isinstance(x, str):
                out.append(self.R(x))
            else:
                out.extend(self._rs(x))
        return out

    def op(self, eng, fn, reads=(), writes=(), dma=None):
        o = Op(eng, fn)
        o.tag = self.stage
        reads = self._rs(reads)
        writes = self._rs(writes)
        deps = {}

        def add(p, kind):
            if p is None or p is o:
                return
            if p.dma_key is None and p.eng == eng and eng == "pe":
                return
            deps[id(p)] = p

        for r in reads:
            add(r.last_w, "raw")
        for w in writes:
            add(w.last_w, "waw")
            for rd in w.readers:
                add(rd, "war")
        for r in reads:
            if dma is None:
                r.readers = [q for q in r.readers if q.dma_key is not None or q.eng != eng]
            r.readers.append(o)
        for w in writes:
            w.last_w = o
            w.readers = []
        o.deps = list(deps.values())
        for p in o.deps:
            p.has_dep = True
        if dma is not None:
            c = self.dma_cnt.get(dma, 0) + 16
            self.dma_cnt[dma] = c
            o.dma_key = dma
            o.dma_val = c
        self.ops[eng].append(o)
        return o

    def emit(self, stack):
        nc = self.nc
        nsig = {}
        for e in ENGS:
            k = 0
            for o in self.ops[e]:
                if o.dma_key is None and o.has_dep:
                    k += 1
                    o.sig = k
            nsig[e] = k
        esem = {}
        for e in ENGS:
            ne = (nsig[e] + EPOCH - 1) // EPOCH
            esem[e] = [stack.enter_context(nc.semaphore(f"s_{e}{i}")) for i in range(ne)]
        dsem = {k: stack.enter_context(nc.semaphore(f"d_{k}")) for k in self.dma_cnt}
        self.stats = {e: [len(self.ops[e]), nsig[e], 0] for e in ENGS}
        block = stack.enter_context(nc.Block())

        def run(e, engine):
            waited = {}
            for o in self.ops[e]:
                need = {}
                for p in o.deps:
                    if p.dma_key is not None:
                        key = ("d", p.dma_key)
                        val = p.dma_val
                    else:
                        key = ("e", p.eng)
                        val = p.sig
                    if need.get(key, 0) < val:
                        need[key] = val
                for key, val in need.items():
                    if waited.get(key, 0) >= val:
                        continue
                    waited[key] = val
                    if key[0] == "d":
                        sem = dsem[key[1]]
                        wv = val
                    else:
                        ep = (val - 1) // EPOCH
                        sem = esem[key[1]][ep]
                        wv = val - ep * EPOCH
                    engine.wait_ge(sem, wv)
                    self.stats[e][2] += 1
                if o.fn is None:
                    continue
                ins = o.fn(engine)
                if o.dma_key is not None:
                    ins.then_inc(dsem[o.dma_key], 16)
                elif o.sig is not None:
                    ep = (o.sig - 1) // EPOCH
                    ins.then_inc(esem[e][ep], 1)

        @block.tensor
        def _(eng):
            run("pe", eng)

        @block.scalar
        def _(eng):
            run("act", eng)

        @block.vector
        def _(eng):
            run("dve", eng)

        @block.gpsimd
        def _(eng):
            run("pool", eng)

        @block.sync
        def _(eng):
            run("sp", eng)


def build_nc(n_groups=NG, debug=False, totals=None):
    nc = bass.Bass("TRN2", target_bir_lowering=False)
    dram = lambda n, s, d=F32, kind="ExternalInput": nc.dram_tensor(n, s, d, kind=kind).ap()
    x_d = dram("x", [S, D])
    win_d = dram("w_in", [D, PROJ])
    wout_d = dram("w_out", [D, D])
    w1_d = dram("w_ff1", [D, DFF])
    w2_d = dram("w_ff2", [DFF, D])
    gT_d = dram("gT", [128, 24])
    gf_d = dram("gf", [1, D])
    sinks_d = dram("sinks", [1, 8])
    tbraw_d = dram("tbraw", [8, 128, 256])
    ta_d = dram("ta", [8, 128, 256])
    cst_d = dram("cst", [128, 128 + 256 + 256 + 64])
    out_d = dram("out", [S, D], F32, kind="ExternalOutput")
    w1bf_d = dram("w1bf", [16, 128, 8, 256], BF16, kind="Internal")
    w2bf_d = dram("w2bf", [16, 128, 4, 512], BF16, kind="Internal")
    dbg = {}
    if debug:
        dbg["qp"] = dram("dbg_qp", [8, 128, 512], F32, kind="ExternalOutput")
        dbg["kt"] = dram("dbg_kt", [128, 5, 1024], F32, kind="ExternalOutput")
        dbg["v"] = dram("dbg_v", [128, 8 * 10 * 65], F32, kind="ExternalOutput")
        dbg["y"] = dram("dbg_y", [128, 4 * 1024], F32, kind="ExternalOutput")
        dbg["h"] = dram("dbg_h", [128, 4 * 1024], F32, kind="ExternalOutput")

    with ExitStack() as st:
        sb = lambda n, s, d: st.enter_context(nc.sbuf_tensor(n, s, d))
        win_sb = sb("win_sb", [128, 8, PROJ], BF16)
        wout_sb = sb("wout_sb", [128, 8, D], BF16)
        wst = [sb(f"wst{i}", [128, 2048], BF16) for i in range(4)]
        xh = [sb(f"xh{i}", [128, 4, D], F32) for i in range(2)]
        xn = [sb(f"xn{i}", [128, D], BF16) for i in range(2)]
        xtmp = [sb(f"xtmp{i}", [128, D], F32) for i in range(2)]
        nT = sb("nT", [128, 8, 512], BF16)
        n2T = sb("n2T", [128, 8, 512], BF16)
        QP = [sb(f"qp{i}", [128, 512], BF16) for i in range(8)]
        KT = sb("KT", [128, 5, 1024], BF16)
        V = sb("V", [128, 8, 10, 65], BF16)
        PT = [sb(f"pt{i}", [128, 2560], BF16) for i in range(2)]
        ybuf = sb("ybuf", [128, 4, D], BF16)
        uT = sb("uT", [128, 16, 512], BF16)
        rtmp = [sb(f"rtmp{i}", [128, 512], BF16) for i in range(2)]
        TA = sb("TA", [128, 8, 256], BF16)
        TB = sb("TB", [128, 8, 256], BF16)
        ident = sb("ident", [128, 128], BF16)
        mask9 = sb("mask9", [128, 64], BF16)
        gf = sb("gf_sb", [128, D], F32)
        gT = sb("gT_sb", [128, 24], F32)
        esink = sb("esink", [128, 8], F32)
        mhalf = sb("mhalf", [128, 8], F32)
        stat = [sb(f"stat{i}", [128, 40], F32) for i in range(2)]
        rl = [sb(f"rl{i}", [128, 4], F32) for i in range(2)]
        bank = [st.enter_context(nc.psum_tensor(f"bank{i}", [128, 512], F32)) for i in range(8)]
        PA = [0, 1, 2, 3]
        PB = [4, 5, 6, 7]

        P = Prog(nc)
        op = P.op

        WBLK = {"KAV": (512, 768), "KB": (1280, 1792), "VB": (1792, 2304), "QA": (0, 512), "QB": (768, 1280)}

        def win_load(name):
            c0, c1 = WBLK[name]
            op("pool", DMA(win_sb[:, :, c0:c1], win_d[:, c0:c1].rearrange("(k p) c -> p k c", p=128)),
               writes=[f"win_{name}"], dma=f"c_win_{name}")

        def win_res(col0):
            for name, (c0, c1) in WBLK.items():
                if c0 <= col0 < c1:
                    return f"win_{name}"
            raise KeyError(col0)

        tbtmp = uT[:].rearrange("p a b -> p (a b)").bitcast(F32)
        UT_ALL = [f"uT{c}" for c in range(16)]
        tbraw = tbtmp[:, 0:2048].rearrange("p (h u) -> p h u", u=256)
        vneg = tbtmp[:, 2048:2560]

        def setup_early():
            op("sp", DMA(gT[:], gT_d), writes=["gT"], dma="c_gT")
            op("pool", MS(mhalf[:], -0.5), writes=["mhalf"])
            op("pool", DMA(ident[:], cst_d[:, 0:128]), writes=["ident"], dma="c_id")
            win_load("KAV")
            win_load("KB")

        def setup_mid():
            x_loads(0)
            win_load("VB")
            win_load("QA")
            op("pool", MS(V[:].rearrange("p a b c -> p (a b c)"), 1.0),
               writes=[f"V{i}{sfx}" for i in range(8) for sfx in ("_a", "_b")])
            for i in range(8):
                op("pool", MS(QP[i][:], 0.0), writes=[f"qp{i}"])
            op("pool", DMA(TA[:], ta_d.rearrange("h p u -> p h u")), writes=["TA"], dma="c_ta")
            op("pool", DMA(mask9[:], cst_d[:, 640:704]), writes=["mask9"], dma="c_m9")
            win_load("QB")
            op("sp", DMA(esink[:], sinks_d.partition_broadcast(128)), writes=["esink"], dma="c_sink")
            op("sp", DMA(gf[:], gf_d.partition_broadcast(128)), writes=["gf"], dma="c_gf")
            op("sp", DMA(tbraw, tbraw_d.rearrange("h p u -> p h u")), writes=UT_ALL, dma="c_tb")
            op("sp", DMA(vneg, cst_d[:, 128:640]), writes=["vneg"], dma="c_vn")

        def setup_tb():
            for h in range(8):
                op("dve", TS(tbraw[:, h, :], tbraw[:, h, :], tbraw[:, h, 255:256], None, ALU.subtract),
                   reads=UT_ALL, writes=UT_ALL)
                op("dve", TT(tbraw[:, h, :], tbraw[:, h, :], vneg[:, 0:256], ALU.mult),
                   reads=UT_ALL + ["vneg"], writes=UT_ALL)
                op("dve", TT(TB[:, h, :], tbraw[:, h, :], vneg[:, 256:512], ALU.add),
                   reads=UT_ALL + ["vneg"], writes=["TB"])
            op("act", ACTF(esink[:], esink[:], AF.Exp), reads=["esink"], writes=["esink"])

        def setup_late():
            for k in range(8):
                op("pool", DMA(wout_sb[:, k, :], wout_d[k * 128:(k + 1) * 128, :]), writes=[f"wout{k}"], dma=f"c_wout{k}")

        stream = []
        for ffh in range(2):
            for j in range(8):
                stream.append(("w1", ffh * 8 + j))
            for dmh in range(2):
                for q in range(4):
                    stream.append(("w2", (ffh * 2 + dmh) * 4 + q))
        NST = len(stream)
        st_state = {"next": 0}

        def stream_ensure(upto):
            upto = min(upto, NST * n_groups)
            while st_state["next"] < upto:
                s_ = st_state["next"]
                kind, idx = stream[s_ % NST]
                slot = s_ % 4
                if kind == "w1":
                    scr = w1bf_d[idx].rearrange("p k c -> p (k c)")
                    rd = f"w1bf{idx}"
                    src32 = w1_d[:, idx * 256:(idx + 1) * 256].rearrange("(k p) c -> p k c", p=128)
                    dst3 = wst[slot][:].rearrange("p (k c) -> p k c", c=256)
                else:
                    scr = w2bf_d[idx].rearrange("p r c -> p (r c)")
                    rd = f"w2bf{idx}"
                    ffh, dmh, q = idx // 8, (idx // 4) % 2, idx % 4
                    r0 = (ffh * 16 + q * 4) * 128
                    src32 = w2_d[r0:r0 + 512, dmh * 512:(dmh + 1) * 512].rearrange("(r p) c -> p r c", p=128)
                    dst3 = wst[slot][:].rearrange("p (r c) -> p r c", c=512)
                if s_ < NST:
                    op("pool", DMA(dst3, src32), writes=[f"wst{slot}"], dma=f"wstc{slot}")
                    if n_groups > 1:
                        op("sp", DMA(scr, wst[slot][:]), reads=[f"wst{slot}"], writes=[rd], dma=f"wb{slot}")
                else:
                    op("sp", DMA(wst[slot][:], scr), reads=[rd], writes=[f"wst{slot}"], dma=f"wst{slot}")
                st_state["next"] += 1

        rot = {"A": 0, "xn": 0}
        NXN = len(xn)

        def next_bankA():
            b = PA[rot["A"] % 4]
            rot["A"] += 1
            return b

        xn_busy = set()

        def xn_slot():
            for _ in range(NXN):
                s_ = rot["xn"] % NXN
                rot["xn"] += 1
                if s_ not in xn_busy:
                    xn_busy.add(s_)
                    return s_
            raise AssertionError("no free xn slot")

        def XN(s_):
            return [f"xn{s_}_0", f"xn{s_}_1"]

        def XH(par, tb):
            return [f"xh{par}_{tb}_0", f"xh{par}_{tb}_1"]

        def NTR(name, ks, tbs):
            return [f"{name}{k}_{t}" for k in ks for t in tbs]

        def rstd_from_ssq(sq_ap, r_ap, n, res_sq, res_r):
            k = sq_ap.shape[1]
            op("pool", TS(sq_ap, sq_ap, 1.0 / n, EPS, ALU.mult, ALU.add), reads=[res_sq], writes=[res_sq])
            op("pool", TT(r_ap, sq_ap, mhalf[:, 0:k], ALU.pow), reads=[res_sq, "mhalf"], writes=[res_r])

        def transpose_block(src_bf, src_res, dst, dname, tb, gcol):
            b = PA[rot["A"] % 2]
            rot["A"] += 1
            pb = bank[b][:].bitcast(BF16)
            for k in range(8):
                op("pe", TR(pb[:, k * 128:(k + 1) * 128], src_bf[:, k * 128:(k + 1) * 128], ident[:]),
                   reads=[src_res, "ident"], writes=[f"bank{b}"])
            for k in range(8):
                op("dve", TS(dst[:, k, tb * 128:(tb + 1) * 128], pb[:, k * 128:(k + 1) * 128],
                             gT[:, gcol + k:gcol + k + 1], None, ALU.mult),
                   reads=[f"bank{b}", "gT"], writes=[f"{dname}{k}_{tb}"])

        out_dmas = []
        ALLTB = range(4)

        xloaded = set()
        n2T_gen = {"g": -1}

        def x_loads(g):
            par = g % 2
            for tb in range(4):
                xloaded.add((g, tb))
                op("sp", DMA(xh[par][:, tb, :], x_d[g * 512 + tb * 128:g * 512 + (tb + 1) * 128, :]),
                   writes=XH(par, tb), dma=f"x{par}{tb}")

        def FN_a(g):
            par = g % 2
            sg = stat[par]
            junk = uT[:, 0:2, :].rearrange("p a b -> p (a b)")
            for tb in range(4):
                op("act", ACTF(junk, xh[par][:, tb, :], AF.Square, accum_out=sg[:, 32 + tb:33 + tb]),
                   reads=XH(par, tb), writes=["uT0", "uT1", f"st{par}_g{tb}"])

        def FN_b(g, tbs=None):
            par = g % 2
            sg = stat[par]
            for tb in (range(4) if tbs is None else tbs):
                rstd_from_ssq(sg[:, 32 + tb:33 + tb], sg[:, 36 + tb:37 + tb], D, f"st{par}_g{tb}", f"st{par}_h{tb}")

        def FN_c(g, tbs=None):
            par = g % 2
            sg = stat[par]
            xg = xh[par]
            t0 = g * 512
            for tb in (range(4) if tbs is None else tbs):
                op("dve", STT(xg[:, tb, :], xg[:, tb, :], sg[:, 36 + tb:37 + tb], gf[:], ALU.mult, ALU.mult),
                   reads=XH(par, tb) + [f"st{par}_h{tb}", "gf"], writes=XH(par, tb))

        def FN_d(g, tbs=None):
            par = g % 2
            xg = xh[par]
            t0 = g * 512
            for tb in (range(4) if tbs is None else tbs):
                o = op("sp", DMA(out_d[t0 + tb * 128:t0 + (tb + 1) * 128, :], xg[:, tb, :]),
                       reads=XH(par, tb), dma=f"o{par}{tb}")
                out_dmas.append(o)

        def P1(g):
            par = g % 2
            xg = xh[par]
            sg = stat[par]
            t0 = g * 512
            P.stage = f"A{g}"
            slots = {}

            def xt_load(tb):
                xt = tb % 2
                q_ = "sp" if tb < 2 else "pool"
                op(q_, DMA(xtmp[xt][:], x_d[t0 + tb * 128:t0 + (tb + 1) * 128, :]),
                   writes=[f"xtmp{xt}"], dma=f"xt{xt}_{q_}")

            def preA(tb):
                xs = xn_slot()
                slots[tb] = xs
                xt = tb % 2
                if tb < 2:
                    xt_load(tb)
                op("act", ACTF(xn[xs][:], xtmp[xt][:], AF.Square, accum_out=sg[:, tb:tb + 1]),
                   reads=[f"xtmp{xt}"], writes=XN(xs) + [f"st{par}_a{tb}"])
                rstd_from_ssq(sg[:, tb:tb + 1], sg[:, 4 + tb:5 + tb], D, f"st{par}_a{tb}", f"st{par}_b{tb}")
                if tb % 2 == 0:
                    op("act", ACTF(xn[xs][:], xtmp[xt][:], AF.Copy, scale=sg[:, 4 + tb:5 + tb]),
                       reads=[f"xtmp{xt}", f"st{par}_b{tb}"], writes=XN(xs))
                else:
                    op("pool", TS(xn[xs][:], xtmp[xt][:], sg[:, 4 + tb:5 + tb], 1.0, ALU.mult, ALU.mult),
                       reads=[f"xtmp{xt}", f"st{par}_b{tb}"], writes=XN(xs))
                if tb + 2 < 4:
                    xt_load(tb + 2)

            def TA_(tb):
                xs = slots[tb]
                transpose_block(xn[xs], XN(xs), nT, "nT", tb, 0)
                xn_busy.discard(xs)

            preA(0)
            preA(1)
            for _ in range(3):
                yield 6000
            TA_(0)
            preA(2)
            yield 5000
            TA_(1)
            preA(3)
            yield 5000
            TA_(2)
            yield 4000
            TA_(3)
            yield 2048

            P.stage = f"B{g}"

            def proj_fm(col0, evac):
                b = next_bankA()
                for k in range(8):
                    op("pe", MM(bank[b][:], win_sb[:, k, col0:col0 + 128], nT[:, k, :], k == 0, k == 7),
                       reads=NTR("nT", [k], ALLTB) + [win_res(col0)], writes=[f"bank{b}"])
                evac(b)

            def evac_k(chunk):
                def f(b):
                    op("act", ACTF(KT[:, chunk, par * 512:(par + 1) * 512], bank[b][:], AF.Copy),
                       reads=[f"bank{b}"], writes=[f"KT{par}_{chunk}"])
                return f

            def evac_q(t_lo, t_hi):
                def f(b):
                    op("act", ACTF(QP[t_lo][0:64, :], bank[b][0:64, :], AF.Copy, scale=0.125),
                       reads=[f"bank{b}"], writes=[f"qp{t_lo}"])
                    op("act", ACTF(QP[t_hi][64:128, :], bank[b][64:128, :], AF.Copy, scale=0.125),
                       reads=[f"bank{b}"], writes=[f"qp{t_hi}"])
                return f

            proj_fm(C_KA, evac_k(0))
            yield 4096
            for m in range(4):
                proj_fm(C_KB + m * 128, evac_k(1 + m))
                yield 4096
            for tb in range(4):
                slot = (g * 4 + tb) % 8
                b1 = next_bankA()
                b2 = next_bankA()
                for k in range(8):
                    op("pe", MM(bank[b1][:, 0:128], nT[:, k, tb * 128:(tb + 1) * 128],
                                win_sb[:, k, C_VA:C_VA + 128], k == 0, k == 7),
                       reads=[f"nT{k}_{tb}", "win_KAV"], writes=[f"bank{b1}"])
                for k in range(8):
                    op("pe", MM(bank[b2][:], nT[:, k, tb * 128:(tb + 1) * 128],
                                win_sb[:, k, C_VB:C_VB + 512], k == 0, k == 7),
                       reads=[f"nT{k}_{tb}", "win_VB"], writes=[f"bank{b2}"])
                op("dve", CP(V[:, slot, 0:2, 0:64], bank[b1][:, 0:128].rearrange("p (h d) -> p h d", d=64)),
                   reads=[f"bank{b1}"], writes=[f"V{slot}_a"])
                op("dve", CP(V[:, slot, 2:10, 0:64], bank[b2][:].rearrange("p (h d) -> p h d", d=64)),
                   reads=[f"bank{b2}"], writes=[f"V{slot}_b"])
                yield 5120
            for j in range(4):
                proj_fm(C_QA + j * 128, evac_q(j, 4 + j))
                yield 4096

            def attention(mixer):
                W = 3 if mixer == "A" else 9
                TX = TA if mixer == "A" else TB
                txres = "TA" if mixer == "A" else "TB"
                vsuf = "_a" if mixer == "A" else "_b"
                back = 1 if mixer == "A" else 4
                kbs = [kb for kb in range(4 * g - back, 4 * g + 4) if kb >= 0]
                info = {}
                off = 0
                for ki, kb in enumerate(kbs):
                    cs = max(2 * kb, 8 * g)
                    ce = min(2 * kb + W, 8 * g + 7)
                    u0 = 64 * (cs - 2 * kb)
                    u1 = 64 * (ce + 1 - 2 * kb)
                    info[kb] = (u0, u1, off, 64 * (cs - 8 * g), ki)
                    off += u1 - u0
                sbank = [PA[0], PA[1]]
                obank = [PA[2], PA[3]]
                cnt = {"s": 0}

                def qk(h, fill=()):
                    fill = list(fill)
                    nkb = len(kbs)
                    ptp = h % 2
                    if mixer == "A":
                        qt = h
                        kchunk = 0
                    else:
                        qt = (h // 2) if h % 2 == 0 else 4 + h // 2
                        kchunk = 1 + h // 2
                    for kidx, kb in enumerate(kbs):
                        u0, u1, poff, qc0, ki = info[kb]
                        n = u1 - u0
                        b = sbank[cnt["s"] % 2]
                        cnt["s"] += 1
                        kcol = (kb % 8) * 128
                        kpar = (kb // 4) % 2
                        mms = [(bank[b][:, 0:n], KT[:, kchunk, kcol:kcol + 128], QP[qt][:, qc0:qc0 + n],
                                [f"KT{kpar}_{kchunk}", f"qp{qt}"])]
                        if u0 < 256:
                            ub1 = min(u1, 256)
                            mms.append((bank[b][:, 0:ub1 - u0], ident[:], TX[:, h, u0:ub1], ["ident", txres]))
                        if mixer == "B" and u1 == 640:
                            mms.append((bank[b][:, 576 - u0:640 - u0], ident[:], mask9[:], ["ident", "mask9"]))
                        cost = 0
                        for i, (o_, l_, r_, rd) in enumerate(mms):
                            op("pe", MM(o_, l_, r_, i == 0, i == len(mms) - 1), reads=rd, writes=[f"bank{b}"])
                            cost += o_.shape[1]
                        op("act", ACTF(PT[ptp][:, poff:poff + n], bank[b][:, 0:n], AF.Exp),
                           reads=[f"bank{b}"], writes=[f"pt{ptp}_{ki}"])
                        left = nkb - kidx
                        take = (len(fill) + left - 1) // left
                        for f in fill[:take]:
                            f()
                        fill = fill[take:]
                        yield cost + 100 * take
                    for f in fill:
                        f()

                def pv(h):
                    ptp = h % 2
                    b = obank[h % 2]
                    hv = (h // 4) if mixer == "A" else 2 + h
                    ycol = (0 if mixer == "A" else 512) + h * 64
                    ems = []
                    for qbl in range(4):
                        qb = 4 * g + qbl
                        ks = [kb for kb in range(qb - back, qb + 1) if kb >= 0]
                        for i, kb in enumerate(ks):
                            u0, u1, poff, qc0, ki = info[kb]
                            c0 = poff + 128 * (qb - kb) - u0
                            slot = kb % 8

                            def em(qbl=qbl, c0=c0, slot=slot, ki=ki, first=(i == 0), last=(i == len(ks) - 1)):
                                op("pe", MM(bank[b][:, qbl * 65:(qbl + 1) * 65], PT[ptp][:, c0:c0 + 128],
                                            V[:, slot, hv, :], first, last),
                                   reads=[f"pt{ptp}_{ki}", f"V{slot}{vsuf}"], writes=[f"bank{b}"])
                            ems.append(em)
                    ems.append(lambda: pv_epilogue(h, b, ycol))
                    return ems

                def pv_epilogue(h, b, ycol):
                    ov = bank[b][:, 0:260].rearrange("p (q d) -> p q d", d=65)
                    lsum = ov[:, :, 64:65].rearrange("p q o -> p (q o)")
                    rli = rl[h % 2]
                    rres = f"rl{h % 2}"
                    if mixer == "A":
                        op("dve", TS(rli[:], lsum, esink[:, h:h + 1], None, ALU.add),
                           reads=[f"bank{b}", "esink"], writes=[rres])
                        op("dve", RCP(rli[:], rli[:]), reads=[rres], writes=[rres])
                    else:
                        op("dve", RCP(rli[:], lsum), reads=[f"bank{b}"], writes=[rres])
                    op("dve", TT(ybuf[:, :, ycol:ycol + 64], ov[:, :, 0:64],
                                 rli[:].unsqueeze(2).to_broadcast([128, 4, 64]), ALU.mult),
                       reads=[f"bank{b}", rres], writes=[f"y{mixer}{h}"])

                yield from qk(0)
                for h in range(1, 8):
                    yield from qk(h, pv(h - 1))
                ems = pv(7)
                for f in ems:
                    f()
                yield 100 * len(ems)

            P.stage = f"CA{g}"
            yield from attention("A")
            P.stage = f"QB{g}"
            for m in range(4):
                proj_fm(C_QB + m * 128, evac_q(m, 4 + m))
                yield 4096
            P.stage = f"CB{g}"
            yield from attention("B")

            P.stage = f"D{g}"
            yslots = {}
            hslots = {}

            def preY(tb):
                xs = xn_slot()
                yslots[tb] = xs
                for m in range(2):
                    mx = "AB"[m]
                    op("act", ACTF(xn[xs][:, m * 512:(m + 1) * 512], ybuf[:, tb, m * 512:(m + 1) * 512],
                                   AF.Square, accum_out=sg[:, 8 + 2 * tb + m:9 + 2 * tb + m]),
                       reads=[f"y{mx}{h}" for h in range(8)], writes=[f"xn{xs}_{m}", f"st{par}_c{tb}"])
                rstd_from_ssq(sg[:, 8 + 2 * tb:10 + 2 * tb], sg[:, 16 + 2 * tb:18 + 2 * tb], 512,
                              f"st{par}_c{tb}", f"st{par}_d{tb}")
                for m in range(2):
                    mx = "AB"[m]
                    op("pool", TS(xn[xs][:, m * 512:(m + 1) * 512], ybuf[:, tb, m * 512:(m + 1) * 512],
                                  sg[:, 16 + 2 * tb + m:17 + 2 * tb + m], 1.0, ALU.mult, ALU.mult),
                       reads=[f"y{mx}{h}" for h in range(8)] + [f"st{par}_d{tb}"], writes=[f"xn{xs}_{m}"])

            def TY(tb):
                xs = yslots[tb]
                transpose_block(xn[xs], XN(xs), nT, "nT", tb, 8)
                xn_busy.discard(xs)

            def OP(tb):
                assert (g, tb) in xloaded, ("x for the residual not loaded yet", g, tb)
                for dmh in range(2):
                    b = next_bankA()
                    for k in range(8):
                        op("pe", MM(bank[b][:], nT[:, k, tb * 128:(tb + 1) * 128],
                                    wout_sb[:, k, dmh * 512:(dmh + 1) * 512], k == 0, k == 7),
                           reads=[f"nT{k}_{tb}", f"wout{k}"], writes=[f"bank{b}"])
                    hs = xg[:, tb, dmh * 512:(dmh + 1) * 512]
                    op("dve", TT(hs, bank[b][:], hs, ALU.add),
                       reads=[f"bank{b}", f"xh{par}_{tb}_{dmh}"], writes=[f"xh{par}_{tb}_{dmh}"])

            def preH(tb):
                xs = xn_slot()
                hslots[tb] = xs
                op("act", ACTF(xn[xs][:], xg[:, tb, :], AF.Square, accum_out=sg[:, 24 + tb:25 + tb]),
                   reads=XH(par, tb), writes=XN(xs) + [f"st{par}_e{tb}"])
                rstd_from_ssq(sg[:, 24 + tb:25 + tb], sg[:, 28 + tb:29 + tb], D, f"st{par}_e{tb}", f"st{par}_f{tb}")
                if tb % 2 == 0:
                    op("act", ACTF(xn[xs][:], xg[:, tb, :], AF.Copy, scale=sg[:, 28 + tb:29 + tb]),
                       reads=XH(par, tb) + [f"st{par}_f{tb}"], writes=XN(xs))
                else:
                    op("pool", TS(xn[xs][:], xg[:, tb, :], sg[:, 28 + tb:29 + tb], 1.0, ALU.mult, ALU.mult),
                       reads=XH(par, tb) + [f"st{par}_f{tb}"], writes=XN(xs))

            def TH(tb):
                n2T_gen["g"] = g
                xs = hslots[tb]
                transpose_block(xn[xs], XN(xs), n2T, "n2T", tb, 16)
                xn_busy.discard(xs)

            preY(0)
            preY(1)
            yield 3000
            yield 3000
            TY(0)
            preY(2)
            yield 2048
            TY(1)
            yield 2048
            OP(0)
            preY(3)
            yield 8192
            TY(2)
            yield 2048
            OP(1)
            preH(0)
            yield 8192
            TY(3)
            yield 2048
            OP(2)
            preH(1)
            yield 8192
            TH(0)
            yield 2048
            OP(3)
            preH(2)
            yield 8192
            TH(1)
            preH(3)
            yield 2048
            yield 3000
            TH(2)
            yield 2048
            TH(3)
            yield 2048

        def P2(g):
            assert totals is None or n2T_gen["g"] == g, ("P2 started before its n2T was complete", g, n2T_gen)
            par = g % 2
            xg = xh[par]
            sg = stat[par]
            t0 = g * 512
            sbase = g * NST
            fcnt = 0
            head = {"n": 0}

            def head_hook():
                head["n"] += 1
                keep = P.stage
                P.stage = f"FN{g - 1}"
                if head["n"] == 1 and g > 0:
                    FN_b(g - 1)
                if 2 <= head["n"] <= 5 and g > 0:
                    FN_c(g - 1, [head["n"] - 2])
                if head["n"] == 7:
                    if g > 0:
                        FN_d(g - 1)
                    if g + 1 < n_groups:
                        x_loads(g + 1)
                    xready[g + 1] = True
                P.stage = keep

            for ffh in range(2):
                P.stage = f"F1_{g}"
                for j in range(8):
                    si = sbase + ffh * 16 + j
                    stream_ensure(si + 4)
                    slot = si % 4
                    w1v = wst[slot][:].rearrange("p (k c) -> p k c", c=256)
                    for cc in range(2):
                        c = j * 2 + cc
                        if ffh == 0:
                            head_hook()
                        b = PB[fcnt % 4]
                        rs = fcnt % 2
                        fcnt += 1
                        assert totals is None or n2T_gen["g"] == g, ("n2T overwritten before FFN1 read it", g, n2T_gen)
                        for k in range(8):
                            op("pe", MM(bank[b][:], w1v[:, k, cc * 128:(cc + 1) * 128], n2T[:, k, :], k == 0, k == 7),
                               reads=NTR("n2T", [k], ALLTB) + [f"wst{slot}"], writes=[f"bank{b}"])
                            if k < 7:
                                yield 512
                        op("dve", TS(rtmp[rs][:], bank[b][:], 0.0, None, ALU.max), reads=[f"bank{b}"], writes=[f"rtmp{rs}"])
                        op("pool", TT(uT[:, c, :], rtmp[rs][:], rtmp[rs][:], ALU.mult),
                           reads=[f"rtmp{rs}"], writes=[f"uT{c}"])
                        yield 512
                P.stage = f"F2_{g}"
                for dmh in range(2):
                    for q in range(4):
                        si = sbase + ffh * 16 + 8 + dmh * 4 + q
                        stream_ensure(si + 4)
                        slot = si % 4
                        w2v = wst[slot][:].rearrange("p (r c) -> p r c", c=512)
                        for tb in range(4):
                            b = PB[tb]
                            for r in range(4):
                                c = q * 4 + r
                                op("pe", MM(bank[b][:], uT[:, c, tb * 128:(tb + 1) * 128], w2v[:, r, :],
                                            q == 0 and r == 0, q == 3 and r == 3),
                                   reads=[f"uT{c}", f"wst{slot}"], writes=[f"bank{b}"])
                                if r < 3:
                                    yield 512
                            if q == 3:
                                hs = xg[:, tb, dmh * 512:(dmh + 1) * 512]
                                op("dve", TT(hs, bank[b][:], hs, ALU.add),
                                   reads=[f"bank{b}", f"xh{par}_{tb}_{dmh}"], writes=[f"xh{par}_{tb}_{dmh}"])
                            yield 512
            P.stage = f"FN{g}"
            FN_a(g)
            yield 0

        def drain(gen):
            for _ in gen:
                pass

        xready = {}
        W = {"P1": 228000.0, "P2": 262144.0}
        meas = {}

        def drain(gen, key=None):
            for c in gen:
                if key:
                    meas[key] += c

        LAG = 1.17

        meas["P1"] = []
        meas["P2"] = []
        pos = {"P1": 0.0, "P2": 0.0}

        def chain(fn, key):
            for g in range(n_groups):
                tot = 0.0
                wg = (totals[key][g] if totals else W[key])
                for c in fn(g):
                    tot += c
                    pos[key] = g + min(tot / wg, 1.0)
                    yield c
                pos[key] = g + 1.0
                meas[key].append(tot)

        def run_pipeline(c1, c2):
            s1 = s2 = ""
            d1 = d2 = False
            while not (d1 and d2):
                take2 = (not d2) and (d1 or (pos["P2"] + LAG <= pos["P1"]))
                if take2:
                    P.stage = s2
                    try:
                        next(c2)
                    except StopIteration:
                        d2 = True
                    s2 = P.stage
                else:
                    P.stage = s1
                    try:
                        next(c1)
                    except StopIteration:
                        d1 = True
                    s1 = P.stage

        P.stage = "setup"
        setup_early()
        c1 = chain(P1, "P1")
        c2 = chain(P2, "P2")
        for _ in range(7):
            next(c1)
        P.stage = "setup"
        setup_mid()
        for _ in range(5):
            next(c1)
        P.stage = "setup"
        setup_late()
        setup_tb()
        stream_ensure(4)
        run_pipeline(c1, c2)
        P.stage = "FNlast"
        for tb in range(4):
            FN_b(n_groups - 1, [tb])
            FN_c(n_groups - 1, [tb])
            FN_d(n_groups - 1, [tb])
        fin = op("sp", None)
        fin.deps = list(out_dmas)
        P.emit(st)
        nc._prog_stats = P.stats
        nc._totals = dict(meas)
        nc._prog_tags = {e: [o.tag for o in P.ops[e] if o.fn is not None] for e in ENGS}
    return nc


def _consts():
    s = np.arange(128)[:, None]
    u = np.arange(256)[None, :]
    sc, uc = s // 64, u // 64
    validA = (sc <= uc) & (uc <= sc + 2)
    validB = (sc <= uc)
    slopes = 2.0 ** (-(np.arange(8) + 1.0))
    ta = np.where(validA[None], -slopes[:, None, None] * np.abs(u - s)[None].astype(np.float64), NEG)
    cst = np.zeros((128, 704), np.float32)
    cst[:, 0:128] = np.eye(128, dtype=np.float32)
    cst[:, 128:384] = validB.astype(np.float32)
    cst[:, 384:640] = np.where(validB, 0.0, NEG)
    cst[0:64, 640:704] = NEG
    idx = np.clip(u - s, -128, 128) + 128
    return ta.astype(np.float32), cst, idx


def kernel(x, norm1_g, w_in, sinks_a, rel_bias_b, out_norm_a_g, out_norm_b_g,
           w_out, norm2_g, w_ff1, w_ff2, final_norm_g):
    x = np.asarray(x, np.float32)
    B = x.shape[0]
    ta, cst, idx = _consts()
    w_in0 = np.asarray(w_in, np.float32)[0]
    perm = []
    for j in range(4):
        perm += list(range(j * 64, (j + 1) * 64)) + list(range((4 + j) * 64, (5 + j) * 64))
    cols = np.concatenate([np.array(perm), np.arange(512, PROJ)])
    w_in_p = np.ascontiguousarray(w_in0[:, cols])
    gcat = np.concatenate([np.asarray(norm1_g, np.float32)[0],
                           np.asarray(out_norm_a_g, np.float32)[0], np.asarray(out_norm_b_g, np.float32)[0],
                           np.asarray(norm2_g, np.float32)[0]])
    gT = np.ascontiguousarray(gcat.reshape(24, 128).T)
    tbraw = np.ascontiguousarray(np.asarray(rel_bias_b, np.float32)[0][:, idx])
    shared = {
        "w_in": w_in_p,
        "w_out": np.ascontiguousarray(np.asarray(w_out, np.float32)[0]),
        "w_ff1": np.ascontiguousarray(np.asarray(w_ff1, np.float32)[0]),
        "w_ff2": np.ascontiguousarray(np.asarray(w_ff2, np.float32)[0]),
        "gT": gT,
        "gf": np.asarray(final_norm_g, np.float32).reshape(1, D),
        "sinks": np.asarray(sinks_a, np.float32).reshape(1, 8),
        "tbraw": tbraw,
        "ta": ta,
        "cst": cst,
    }
    nc = build_nc(totals=build_nc()._totals)
    in_maps = [dict(shared, x=np.ascontiguousarray(x[b])) for b in range(B)]
    res = run_bass_kernel_spmd(nc, in_maps, core_ids=list(range(B)))
    return np.stack([np.asarray(r["out"], np.float32) for r in res.results], axis=0)
```
